# Optimizing a Trainium2 kernel written in Bass

```python
import math
import jax, jax.numpy as jnp
from jax import lax
import numpy as np

D_MODEL = 1024
BATCH = 8
SEQ = 2048
DEPTH = 2
DEC_BATCH = 128
DEC_SEQ = 8
PAST_LEN = 16384
PAGE_SIZE = 128

RET_HEADS = 8
RET_DK = 64
RET_DV = 128
RET_QK = RET_HEADS * RET_DK
RET_V = RET_HEADS * RET_DV
RET_CHUNK = 128
ROPE_BASE = 10000.0
D_RNN = 1024
RNN_BLOCKS = 16
RNN_BS = D_RNN // RNN_BLOCKS
CONV_W = 4
RG_C = 8.0
IN_SIZES = (RET_QK, RET_QK, RET_V, RET_V, D_RNN, D_RNN, D_MODEL, D_MODEL)
N_IN = sum(IN_SIZES)
PEER_HEADS = 8
N_KEYS = 128
N_EXPERTS = N_KEYS * N_KEYS
PEER_DKEY = 256
PEER_TOPK = 16
PEER_BLOCK = 256
EPS = 1e-6

kernel_name = 'hybrid_retention_rglru_peer_step'


def rmsnorm(x, g):
    xf = x.astype(jnp.float32)
    y = xf * lax.rsqrt(jnp.mean(xf * xf, axis=-1, keepdims=True) + EPS)
    return (y * g.astype(jnp.float32)).astype(x.dtype)


def rotary(x, pos):
    half = x.shape[-1] // 2
    freq = ROPE_BASE ** (-jnp.arange(half, dtype=jnp.float32) / half)
    ang = pos[:, None] * freq[None, :]
    cos = jnp.cos(ang)[None, :, None, :]
    sin = jnp.sin(ang)[None, :, None, :]
    x1, x2 = x[..., :half], x[..., half:]
    return jnp.concatenate([x1 * cos - x2 * sin, x1 * sin + x2 * cos], axis=-1)


def retention(q, k, v, r0):
    b, t = q.shape[0], q.shape[1]
    c = RET_CHUNK if t % RET_CHUNK == 0 else t
    nc = t // c
    log_g = jnp.log1p(-(2.0 ** (-5.0 - jnp.arange(RET_HEADS, dtype=jnp.float32))))
    idx = jnp.arange(c, dtype=jnp.float32)
    diff = idx[:, None] - idx[None, :]
    mask = jnp.where(diff[None] >= 0, jnp.exp(jnp.maximum(diff, 0.0)[None] * log_g[:, None, None]), 0.0)
    qc = q.reshape(b, nc, c, RET_HEADS, RET_DK)
    kc = k.reshape(b, nc, c, RET_HEADS, RET_DK)
    vc = v.reshape(b, nc, c, RET_HEADS, RET_DV)
    s = jnp.einsum('bnjhd,bnmhd->bnhjm', qc, kc) * mask
    intra = jnp.einsum('bnhjm,bnmhe->bnjhe', s, vc)
    k_w = jnp.exp((c - 1 - idx)[:, None] * log_g[None, :])
    kv = jnp.einsum('bnmhd,bnmhe,mh->nbhde', kc, vc, k_w)
    g_c = jnp.exp(c * log_g)[None, :, None, None]

    def step(r, kv_n):
        return g_c * r + kv_n, r

    r_last, r_prev = lax.scan(step, r0, kv)
    q_w = jnp.exp((idx + 1.0)[:, None] * log_g[None, :])
    cross = jnp.einsum('bnjhd,nbhde,jh->bnjhe', qc, r_prev, q_w)
    o = (intra + cross).reshape(b, t, RET_HEADS, RET_DV)
    return o, r_last


def rglru_branch(xr, buf, h0, conv_w, conv_b, wa, ba, wx, bx, lam):
    b, t = xr.shape[0], xr.shape[1]
    f32 = jnp.float32
    xcat = jnp.concatenate([buf.astype(f32), xr], axis=1)
    cw = conv_w.astype(f32)
    xc = conv_b.astype(f32)[None, None, :]
    for w in range(CONV_W):
        xc = xc + xcat[:, w:w + t] * cw[w]
    new_buf = xcat[:, -(CONV_W - 1):]
    xb = xc.reshape(b, t, RNN_BLOCKS, RNN_BS)
    r = jax.nn.sigmoid(jnp.einsum('btnc,ncd->btnd', xb, wa.astype(f32)).reshape(b, t, D_RNN) + ba.astype(f32))
    i = jax.nn.sigmoid(jnp.einsum('btnc,ncd->btnd', xb, wx.astype(f32)).reshape(b, t, D_RNN) + bx.astype(f32))
    log_a = -RG_C * r * jax.nn.softplus(-lam.astype(f32))
    a = jnp.exp(log_a)
    u = jnp.sqrt(-jnp.expm1(2.0 * log_a)) * (i * xc)
    u = u.at[:, 0].add(a[:, 0] * h0)

    def comb(left, right):
        a1, b1 = left
        a2, b2 = right
        return a1 * a2, a2 * b1 + b2

    _, h = lax.associative_scan(comb, (a, u), axis=1)
    return h, h[:, -1], new_buf


def peer(xn, wq, keys, u_tab, v_tab):
    f32 = jnp.float32
    shp = xn.shape
    xf = xn.reshape(-1, D_MODEL)
    n = xf.shape[0]
    q = (xf @ wq).astype(f32).reshape(n, PEER_HEADS, 2, PEER_DKEY // 2)
    s = jnp.einsum('nhpd,hpkd->nhpk', q, keys.astype(f32))
    top_s, top_i = lax.top_k(s, PEER_TOPK)
    cand = top_s[:, :, 0, :, None] + top_s[:, :, 1, None, :]
    best_s, best_c = lax.top_k(cand.reshape(n, PEER_HEADS, PEER_TOPK * PEER_TOPK), PEER_TOPK)
    i1 = jnp.take_along_axis(top_i[:, :, 0], best_c // PEER_TOPK, axis=-1)
    i2 = jnp.take_along_axis(top_i[:, :, 1], best_c % PEER_TOPK, axis=-1)
    eidx = (i1 * N_KEYS + i2).reshape(n, PEER_HEADS * PEER_TOPK)
    gate = jax.nn.softmax(best_s, axis=-1).reshape(n, PEER_HEADS * PEER_TOPK)
    blk = min(PEER_BLOCK, n)
    nb = -(-n // blk)
    pad = nb * blk - n
    xp = jnp.pad(xf, ((0, pad), (0, 0))).reshape(nb, blk, D_MODEL)
    ep = jnp.pad(eidx, ((0, pad), (0, 0))).reshape(nb, blk, PEER_HEADS * PEER_TOPK)
    gp = jnp.pad(gate, ((0, pad), (0, 0))).reshape(nb, blk, PEER_HEADS * PEER_TOPK)

    def block(args):
        xb, eb, gb = args
        u = jnp.take(u_tab, eb, axis=0)
        hid = jax.nn.gelu(jnp.einsum('nd,nkd->nk', xb, u).astype(f32)) * gb
        v = jnp.take(v_tab, eb, axis=0)
        return jnp.einsum('nk,nkd->nd', hid.astype(xb.dtype), v)

    out = lax.map(block, (xp, ep, gp)).reshape(nb * blk, D_MODEL)[:n]
    return out.reshape(shp)


def trunk(x, r0, h0, buf0, pos0, norm1_g, norm2_g, normf_g, w_in, ret_gn_g, w_ret_out, conv_w, conv_b,
          rg_wa, rg_ba, rg_wx, rg_bx, rg_lambda, w_rnn_out, w_o, peer_wq, peer_keys, peer_u, peer_v):
    f32 = jnp.float32
    b, t = x.shape[0], x.shape[1]
    pos = pos0 + jnp.arange(t, dtype=f32)
    splits = np.cumsum(IN_SIZES)[:-1].tolist()
    new_r, new_h, new_buf = [], [], []
    for l in range(DEPTH):
        xn = rmsnorm(x, norm1_g[l])
        p = xn @ w_in[l]
        q, k, v, g_ret, xr, g_rnn, ga, gb = jnp.split(p, splits, axis=-1)
        qh = rotary(q.reshape(b, t, RET_HEADS, RET_DK).astype(f32), pos)
        kh = rotary(k.reshape(b, t, RET_HEADS, RET_DK).astype(f32), pos) * (RET_DK ** -0.5)
        vh = v.reshape(b, t, RET_HEADS, RET_DV).astype(f32)
        o, r_l = retention(qh, kh, vh, r0[l].astype(f32))
        mu = jnp.mean(o, axis=-1, keepdims=True)
        var = jnp.mean(jnp.square(o - mu), axis=-1, keepdims=True)
        o = ((o - mu) * lax.rsqrt(var + EPS)).reshape(b, t, RET_V) * ret_gn_g[l].astype(f32)
        ret_out = (jax.nn.silu(g_ret.astype(f32)) * o).astype(x.dtype) @ w_ret_out[l]
        h, h_l, buf_l = rglru_branch(xr.astype(f32), buf0[l], h0[l].astype(f32), conv_w[l], conv_b[l],
                                     rg_wa[l], rg_ba[l], rg_wx[l], rg_bx[l], rg_lambda[l])
        rnn_out = (h * jax.nn.gelu(g_rnn.astype(f32))).astype(x.dtype) @ w_rnn_out[l]
        merged = jax.nn.sigmoid(ga) * ret_out + jax.nn.sigmoid(gb) * rnn_out
        x = x + merged @ w_o[l]
        x = x + peer(rmsnorm(x, norm2_g[l]), peer_wq[l], peer_keys[l], peer_u[l], peer_v[l])
        new_r.append(r_l.astype(x.dtype))
        new_h.append(h_l.astype(x.dtype))
        new_buf.append(buf_l.astype(x.dtype))
    y = rmsnorm(x, normf_g)
    return y, jnp.stack(new_r), jnp.stack(new_h), jnp.stack(new_buf)


def setup_inputs(seed: int = 0) -> dict:
    key = jax.random.key(seed)
    ks = jax.random.split(key, 32)
    nrm = jax.random.normal
    f32 = jnp.float32
    lam_u = jax.random.uniform(ks[17], (DEPTH, D_RNN), f32, 0.9, 0.999)
    a_base = lam_u ** (1.0 / RG_C)
    rg_lambda = jnp.log(a_base) - jnp.log1p(-a_base)
    return {
        'x_prompt': nrm(ks[0], (BATCH, SEQ, D_MODEL), f32),
        'x_sample': nrm(ks[1], (DEC_BATCH, DEC_SEQ, D_MODEL), f32),
        'state_ret': 0.5 * nrm(ks[2], (DEPTH, DEC_BATCH, RET_HEADS, RET_DK, RET_DV), f32),
        'state_rnn': 0.5 * nrm(ks[3], (DEPTH, DEC_BATCH, D_RNN), f32),
        'state_conv': nrm(ks[4], (DEPTH, DEC_BATCH, CONV_W - 1, D_RNN), f32),
        'norm1_g': 1.0 + 0.05 * nrm(ks[5], (DEPTH, D_MODEL), f32),
        'norm2_g': 1.0 + 0.05 * nrm(ks[6], (DEPTH, D_MODEL), f32),
        'normf_g': 1.0 + 0.05 * nrm(ks[7], (D_MODEL,), f32),
        'w_in': nrm(ks[8], (DEPTH, D_MODEL, N_IN), f32) * D_MODEL ** -0.5,
        'ret_gn_g': 1.0 + 0.05 * nrm(ks[9], (DEPTH, RET_V), f32),
        'w_ret_out': nrm(ks[10], (DEPTH, RET_V, D_MODEL), f32) * RET_V ** -0.5,
        'conv_w': nrm(ks[11], (DEPTH, CONV_W, D_RNN), f32) * CONV_W ** -0.5,
        'conv_b': 0.01 * nrm(ks[12], (DEPTH, D_RNN), f32),
        'rg_wa': nrm(ks[13], (DEPTH, RNN_BLOCKS, RNN_BS, RNN_BS), f32) * RNN_BS ** -0.5,
        'rg_ba': 0.01 * nrm(ks[14], (DEPTH, D_RNN), f32),
        'rg_wx': nrm(ks[15], (DEPTH, RNN_BLOCKS, RNN_BS, RNN_BS), f32) * RNN_BS ** -0.5,
        'rg_bx': 0.01 * nrm(ks[16], (DEPTH, D_RNN), f32),
        'rg_lambda': rg_lambda,
        'w_rnn_out': nrm(ks[18], (DEPTH, D_RNN, D_MODEL), f32) * D_RNN ** -0.5,
        'w_o': nrm(ks[19], (DEPTH, D_MODEL, D_MODEL), f32) * D_MODEL ** -0.5,
        'peer_wq': nrm(ks[20], (DEPTH, D_MODEL, PEER_HEADS * PEER_DKEY), f32) * D_MODEL ** -0.5,
        'peer_keys': nrm(ks[21], (DEPTH, PEER_HEADS, 2, N_KEYS, PEER_DKEY // 2), f32) * (PEER_DKEY // 2) ** -0.5,
        'peer_u': nrm(ks[22], (DEPTH, N_EXPERTS, D_MODEL), f32) * D_MODEL ** -0.5,
        'peer_v': nrm(ks[23], (DEPTH, N_EXPERTS, D_MODEL), f32) * (PEER_HEADS * PEER_TOPK) ** -0.5,
    }


def reference(x_prompt, x_sample, state_ret, state_rnn, state_conv, norm1_g, norm2_g, normf_g, w_in,
              ret_gn_g, w_ret_out, conv_w, conv_b, rg_wa, rg_ba, rg_wx, rg_bx, rg_lambda, w_rnn_out, w_o,
              peer_wq, peer_keys, peer_u, peer_v):
    bp = x_prompt.shape[0]
    dt = x_prompt.dtype
    zr = jnp.zeros((DEPTH, bp, RET_HEADS, RET_DK, RET_DV), dt)
    zh = jnp.zeros((DEPTH, bp, D_RNN), dt)
    zc = jnp.zeros((DEPTH, bp, CONV_W - 1, D_RNN), dt)
    y_prompt, sr_p, sh_p, sc_p = trunk(x_prompt, zr, zh, zc, 0, norm1_g, norm2_g, normf_g, w_in, ret_gn_g,
                                       w_ret_out, conv_w, conv_b, rg_wa, rg_ba, rg_wx, rg_bx, rg_lambda,
                                       w_rnn_out, w_o, peer_wq, peer_keys, peer_u, peer_v)
    y_sample, sr_s, sh_s, sc_s = trunk(x_sample, state_ret, state_rnn, state_conv, PAST_LEN, norm1_g, norm2_g,
                                       normf_g, w_in, ret_gn_g, w_ret_out, conv_w, conv_b, rg_wa, rg_ba, rg_wx,
                                       rg_bx, rg_lambda, w_rnn_out, w_o, peer_wq, peer_keys, peer_u, peer_v)
    return (y_prompt, y_sample, sr_p, sh_p, sc_p, sr_s, sh_s, sc_s)
```

```python
import math
from contextlib import ExitStack
import numpy as np
import concourse.bass as bass
import concourse.mybir as mybir
from concourse.bass_utils import run_bass_kernel_spmd

F32 = mybir.dt.float32
I32 = mybir.dt.int32
U32 = mybir.dt.uint32
AF = mybir.ActivationFunctionType
ALU = mybir.AluOpType
AX = mybir.AxisListType

NCORES = 8
D = 1024
SEQ = 2048
DEPTH = 2
NSB = 16
ST = 8
PAST = 16384
NH = 8
DK = 64
DV = 128
EPS = 1e-6
NEXP = 16384
NT = 2
NG = 5
BW = 256
NV = 80
NEG = -1.0e30


class MonoSem:
    PERIOD = 24000

    def __init__(self, prog, name):
        self.prog = prog
        self.name = name
        self.handles = {}
        self.count = 0

    def handle(self, idx):
        if idx not in self.handles:
            self.handles[idx] = self.prog.stack.enter_context(
                self.prog.nc.semaphore(f"{self.name}_{idx}"))
        return self.handles[idx]

    def next_inc(self, step):
        P = self.PERIOD
        if (self.count % P) + step > P:
            self.count = (self.count // P + 1) * P
        idx = self.count // P
        self.count += step
        return self.handle(idx), self.count

    def target(self, value):
        P = self.PERIOD
        idx = (value - 1) // P
        return self.handle(idx), value - idx * P


class Buf:
    def __init__(self, prog, name):
        self.prog = prog
        self.name = name
        self.w = None
        self.r = {}
        self.excl = False
        self._dsem = {}

    def dsem(self, eng):
        if eng not in self._dsem:
            self._dsem[eng] = MonoSem(self.prog, "d" + eng + "_" + self.name)
        return self._dsem[eng]


class Reg:
    def __init__(self, prog, name, shape, dtype=F32, psum=False):
        self.t = (prog.psum if psum else prog.sbuf)(name, shape, dtype)
        self.b = prog.buf(name)
        self.b.excl = psum
        self.shape = shape

    def __getitem__(self, k):
        return self.t[k]


def OP(name, *a, **k):
    return (name, a, k)


class Prog:
    ENG = ("pe", "act", "dve", "pool", "sp")

    def __init__(self, nc):
        self.nc = nc
        self.stack = ExitStack()
        self.esem = {e: MonoSem(self, "e_" + e) for e in self.ENG}
        self.seen = {e: {} for e in self.ENG}
        self.ops = {e: [] for e in self.ENG}
        self.nbuf = 0
        self.final_events = []

    def buf(self, name=None):
        self.nbuf += 1
        return Buf(self, name or f"b{self.nbuf}")

    def sbuf(self, name, shape, dtype=F32):
        return self.stack.enter_context(self.nc.sbuf_tensor("sb_" + name, list(shape), dtype))

    def psum(self, name, shape, dtype=F32):
        return self.stack.enter_context(self.nc.psum_tensor("ps_" + name, list(shape), dtype))

    def _need(self, eng, ev, waits):
        if ev is None:
            return
        sem, val = ev
        if self.seen[eng].get(sem, 0) >= val:
            return
        self.seen[eng][sem] = val
        waits.append((sem, val))

    def op(self, eng, fn, reads=(), writes=(), dma=None, partial=False, final=False):
        reads = [getattr(b, "b", b) for b in reads]
        writes = [getattr(b, "b", b) for b in writes]
        dma = getattr(dma, "b", dma)
        waits = []
        mysem = dma.dsem(eng) if dma is not None else self.esem[eng]
        for b in reads:
            self._need(eng, b.w, waits)
            if b.excl:
                for sem, val in b.r.items():
                    if sem is not mysem:
                        self._need(eng, (sem, val), waits)
        for b in writes:
            if b.w is not None and not (partial and b.w[0] is mysem):
                self._need(eng, b.w, waits)
            for sem, val in b.r.items():
                self._need(eng, (sem, val), waits)
        step = 16 if dma is not None else 1
        h, val = mysem.next_inc(step)
        ev = (mysem, val)
        for b in reads:
            if b.r.get(mysem, 0) < val:
                b.r[mysem] = val
        for b in writes:
            b.w = ev
            if not partial:
                b.r = {}
        if final:
            self.final_events.append(ev)
        self.ops[eng].append((waits, fn, h, step))
        return ev

    def finish(self):
        waits = []
        for ev in self.final_events:
            self._need("sp", ev, waits)
        self.ops["sp"].append((waits, None, None, 0))

    def emit(self):
        prog = self

        def replay(ename):
            def run(eng):
                for waits, fn, h, step in prog.ops[ename]:
                    for sem, val in waits:
                        sh, local = sem.target(val)
                        eng.wait_ge(sh, local)
                    if fn is None:
                        continue
                    name, a, k = fn
                    getattr(eng, name)(*a, **k).then_inc(h, step)
            return run

        with self.nc.Block() as block:
            block.sync(replay("sp"))
            block.scalar(replay("act"))
            block.vector(replay("dve"))
            block.gpsimd(replay("pool"))
            block.tensor(replay("pe"))


def host_consts():
    logg = np.log1p(-(2.0 ** (-5.0 - np.arange(NH, dtype=np.float64))))
    freq = 10000.0 ** (-np.arange(32, dtype=np.float64) / 32.0)
    c = {}
    c["ident"] = np.eye(128)
    c["onesdiv"] = np.full((128, 128), 1.0 / 128.0)
    ang = np.arange(SEQ, dtype=np.float64)[:, None] * freq[None, :]
    c["rotp_c"] = np.cos(ang).reshape(16, 128, 32).transpose(1, 0, 2)
    c["rotp_s"] = np.sin(ang).reshape(16, 128, 32).transpose(1, 0, 2)
    tm = np.arange(128) % ST
    bm = np.arange(128) // ST
    angs = (PAST + tm).astype(np.float64)[:, None] * freq[None, :]
    c["rots_c"] = np.cos(angs)
    c["rots_s"] = np.sin(angs)
    m = np.arange(128)
    j = np.arange(128)
    mk = 0.125 * np.exp(-(m[:, None, None] + 1.0) * logg[None, :, None]) * (j[None, None, :] >= m[:, None, None])
    c["mask_p"] = mk
    c["qw_p"] = np.broadcast_to(np.exp((j[None, None, :] + 1.0) * logg[None, :, None]), (64, NH, 128))
    c["kw_p"] = 0.125 * np.exp((127.0 - m[:, None]) * logg[None, :])
    c["gc_p"] = np.broadcast_to(np.exp(128.0 * logg)[None, :], (64, NH))
    same = (bm[:, None] == bm[None, :]) & (tm[None, :] >= tm[:, None])
    mks = 0.125 * np.exp(-(tm[:, None, None] + 1.0) * logg[None, :, None]) * same[:, None, :]
    c["mask_s"] = mks
    c["qw_s"] = np.broadcast_to(np.exp((tm[None, None, :] + 1.0) * logg[None, :, None]), (64, NH, 128))
    c["kw_s"] = 0.125 * np.exp((ST - 1.0 - tm[:, None]) * logg[None, :])
    c["oh"] = (bm[:, None] == np.arange(NSB)[None, :]).astype(np.float64)
    c["iota16"] = np.broadcast_to(np.arange(16, dtype=np.float64)[None, :], (128, 16))
    g8 = [float(np.exp(ST * logg[h])) for h in range(NH)]
    return {k: np.ascontiguousarray(v, dtype=np.float32) for k, v in c.items()}, g8


CONST_SHAPES = {
    "ident": [128, 128], "onesdiv": [128, 128], "rotp_c": [128, 16, 32], "rotp_s": [128, 16, 32],
    "rots_c": [128, 32], "rots_s": [128, 32], "mask_p": [128, NH, 128], "qw_p": [64, NH, 128],
    "kw_p": [128, NH], "gc_p": [64, NH], "mask_s": [128, NH, 128], "qw_s": [64, NH, 128],
    "kw_s": [128, NH], "oh": [128, NSB], "iota16": [128, 16],
}

IN_SHAPES = {
    "xp": [SEQ, D], "xs": [128, D],
    "w_in": [DEPTH, D, 7168], "w_ret": [DEPTH, D, D], "w_rnn": [DEPTH, D, D], "w_o": [DEPTH, D, D],
    "wq": [DEPTH, D, 2048], "keysT": [DEPTH, 128, 16, 128],
    "pu0": [NEXP, D], "pu1": [NEXP, D], "pv0": [NEXP, D], "pv1": [NEXP, D],
    "vecs": [DEPTH, 128, NV], "n2g": [DEPTH, D], "nfg": [D],
    "wab": [DEPTH, 128, 2, 8, 128],
    "sret": [DEPTH, NH, 64, NSB, 128], "sconvT": [DEPTH, 128, 8, NSB, 3], "srnnT": [DEPTH, 128, 8, NSB],
}
OUT_SHAPES = {
    "yp": [SEQ, D], "ys": [128, D],
    "srp": [DEPTH, 64, NH, 128], "shp": [DEPTH, 128, 8], "scp": [DEPTH, 128, 8, 3],
    "srs": [DEPTH, NH, 64, NSB, 128], "shs": [DEPTH, 128, 8, NSB], "scs": [DEPTH, 128, 8, NSB, 3],
}


def build_program(g8, dbg=None):
    nc = bass.Bass("TRN2", target_bir_lowering=False)
    P = Prog(nc)
    din = {}
    for k, s in {**IN_SHAPES, **CONST_SHAPES}.items():
        if dbg is not None and dbg.get("small") and k in ("pu0", "pu1", "pv0", "pv1"):
            s = [128, D]
        din[k] = nc.dram_tensor(k, list(s), F32, kind="ExternalInput").ap()
    dout = {k: nc.dram_tensor(k, list(s), F32, kind="ExternalOutput").ap() for k, s in OUT_SHAPES.items()}
    pu_d = [din["pu0"], din["pu1"]]
    pv_d = [din["pv0"], din["pv1"]]

    TG = NT * 128
    GC = 8 * TG

    def R(name, shape, dtype=F32):
        return Reg(P, name, shape, dtype)

    ident = R("ident", [128, 128])
    onesdiv = R("onesdiv", [128, 128])
    rot = R("rot", [128, 2, NT, 32])
    mask = R("mask", [128, NH, 128])
    qw = R("qw", [64, NH, 128])
    kw = R("kw", [128, NH])
    gc_p = R("gc_p", [64, NH])
    oh = R("oh", [128, NSB])
    iota16 = R("iota16", [128, 16])
    vecs = R("vecs", [128, DEPTH, NV])
    ccol = R("ccol", [128, DEPTH, 2, 8])
    n2g = R("n2g", [128, DEPTH, D])
    wab = R("wab", [128, 2, 8, 128])
    Rst = R("Rst", [64, DEPTH, NH, 128])
    hst = R("hst", [128, DEPTH, 8])
    ctail = R("ctail", [128, DEPTH, 8, 3])
    ssn = R("ssn", [128, 4])
    ssn2 = R("ssn2", [128, 4])
    topv = R("topv", [128, 16, 16])
    topi = R("topi", [128, 16, 16], U32)
    topf = R("topf", [128, 16, 16])
    b8 = R("b8", [128, NH, 16])
    c8 = R("c8", [128, NH, 16], U32)
    cab = R("cab", [128, 2, NH, 16], U32)
    cabf = R("cabf", [128, 2, NH, 16])
    isel = R("isel", [128, 3, NH * 16])
    eidxP = [[R(f"eidx{a}{b}", [128, 128], I32) for b in range(NT)] for a in range(2)]
    gateP = [[R(f"gate{a}{b}", [128, 128]) for b in range(NT)] for a in range(2)]
    gsm = R("gsm", [128, 4, NH, 16])
    hpre = R("hpre", [128, 128])
    hid = R("hid", [128, 128])
    GxP = [R("GxA", [128, NT, D]), R("GxB", [128, NT, D])]
    GxS = R("GxS", [128, 1, D])
    NGROUPS = 16 // NT

    def xbuf(g):
        return GxS if g == NGROUPS else GxP[g % 2]
    Gn = R("Gn", [128, D])
    Gt = [R(f"Gt{i}", [128, D]) for i in range(NG)]
    Ga = R("Ga", [128, GC])
    Gb = R("Gb", [128, 8 * (TG + 3)])
    Gc = R("Gc", [128, GC])
    Gd = R("Gd", [128, GC])
    Ge = R("Ge", [128, GC])
    Gf = R("Gf", [128, GC])
    Gg = R("Gg", [128, GC])
    Gh = R("Gh", [128, GC])
    Tr = [R(f"T{i}", [128, 1024]) for i in range(8)]

    class _View:
        def __init__(self, reg, ap, shape):
            self.t = ap
            self.b = reg.b
            self.shape = shape

        def __getitem__(self, k):
            return self.t[k]
    lam_s = _View(Tr[0], Tr[0][:, 0:8 * DEPTH * 8].rearrange("p (a b) -> p a b", a=8), [128, 8, DEPTH * 8])
    h0s = _View(Gh, Gh[:, 1024:1024 + 8 * NSB].rearrange("p (c b) -> p c b", c=8), [128, 8, NSB])
    hls = _View(Gh, Gh[:, 1280:1280 + 8 * NSB].rearrange("p (c b) -> p c b", c=8), [128, 8, NSB])
    cvs = _View(Gh, Gh[:, 1536:1536 + 8 * NSB * 3].rearrange("p (c b w) -> p c b w", c=8, b=NSB), [128, 8, NSB, 3])
    NWB = 2
    wbuf = [R(f"wb{i}", [128, 8, BW]) for i in range(NWB)]
    psb = [Reg(P, f"pb{i}", [128, 1024], psum=True) for i in range(2)]
    pss = [Reg(P, f"pq{i}", [128, 512], psum=True) for i in range(4)]
    cnt = {"pb": 0, "pq": 0, "wb": 0}

    def big():
        cnt["pb"] += 1
        return psb[cnt["pb"] % 2]

    def small():
        cnt["pq"] += 1
        return pss[cnt["pq"] % 4]

    hook = {"on": False, "credit": 0.0, "rate": 0.7, "fn": None, "busy": False}

    def dve(fn, r, w, **k):
        if hook["on"] and not hook["busy"]:
            hook["credit"] += hook["rate"]
            while hook["credit"] >= 1.0:
                hook["credit"] -= 1.0
                hook["busy"] = True
                hook["fn"](1)
                hook["busy"] = False
        P.op("dve", fn, reads=r, writes=w, **k)

    def act(fn, r, w, **k):
        P.op("act", fn, reads=r, writes=w, **k)

    def pe(fn, r, w, **k):
        P.op("pe", fn, reads=r, writes=w, partial=True, **k)

    def pool(fn, r, w, **k):
        P.op("pool", fn, reads=r, writes=w, **k)

    def load(reg, out_ap, in_ap, partial=False):
        P.op("sp", OP("dma_start", out=out_ap, in_=in_ap), writes=[reg], dma=reg, partial=partial)

    def store(reg, out_ap, in_ap):
        P.op("sp", OP("dma_start", out=out_ap, in_=in_ap), reads=[reg], dma=reg, final=True)

    for nm, reg in [("ident", ident), ("onesdiv", onesdiv), ("gc_p", gc_p), ("oh", oh), ("iota16", iota16)]:
        load(reg, reg[:], din[nm])
    load(vecs, vecs[:], din["vecs"].rearrange("l p n -> p l n"))
    for l in range(DEPTH):
        load(n2g, n2g[:, l, :], din["n2g"][l].partition_broadcast(128), partial=True)
    pool(OP("memset", Rst[:], 0.0), [], [Rst])
    pool(OP("memset", hst[:], 0.0), [], [hst])
    pool(OP("memset", ctail[:], 0.0), [], [ctail])

    def vcol(l, base, c):
        return vecs[:, l, base + c:base + c + 1]

    LW = DEPTH * 8

    def ls(i):
        return lam_s[:, i, :]

    lam_v = vecs[:, :, 56:64]
    lam3 = [lam_s[:, i, :].rearrange("p (l c) -> p l c", l=DEPTH) for i in range(8)]
    dve(OP("tensor_scalar_mul", out=lam3[0], in0=lam_v, scalar1=-1.0), [vecs], [lam_s])
    dve(OP("tensor_max", out=lam3[0], in0=lam3[0], in1=lam_v), [vecs, lam_s], [lam_s])
    act(OP("activation", out=ls(1), in_=ls(0), func=AF.Exp, scale=-1.0), [lam_s], [lam_s])
    dve(OP("tensor_scalar_add", out=ls(2), in0=ls(1), scalar1=2.0), [lam_s], [lam_s])
    dve(OP("reciprocal", out=ls(2), in_=ls(2)), [lam_s], [lam_s])
    dve(OP("tensor_mul", out=ls(2), in0=ls(2), in1=ls(1)), [lam_s], [lam_s])
    dve(OP("tensor_mul", out=ls(3), in0=ls(2), in1=ls(2)), [lam_s], [lam_s])
    dve(OP("tensor_scalar", out=ls(4), in0=ls(3), scalar1=1.0 / 11.0, scalar2=1.0 / 9.0, op0=ALU.mult, op1=ALU.add), [lam_s], [lam_s])
    for coef in (1.0 / 7.0, 1.0 / 5.0, 1.0 / 3.0, 1.0):
        dve(OP("tensor_mul", out=ls(4), in0=ls(4), in1=ls(3)), [lam_s], [lam_s])
        dve(OP("tensor_scalar_add", out=ls(4), in0=ls(4), scalar1=coef), [lam_s], [lam_s])
    dve(OP("tensor_mul", out=ls(4), in0=ls(4), in1=ls(2)), [lam_s], [lam_s])
    dve(OP("tensor_scalar", out=lam3[5], in0=lam_v, scalar1=-1.0, scalar2=0.0, op0=ALU.mult, op1=ALU.max), [vecs], [lam_s])
    dve(OP("scalar_tensor_tensor", out=ls(6), in0=ls(4), scalar=2.0, in1=ls(5), op0=ALU.mult, op1=ALU.add), [lam_s], [lam_s])
    dve(OP("tensor_scalar_mul", out=ccol[:, :, 0, :], in0=lam3[6], scalar1=-8.0), [lam_s], [ccol])
    dve(OP("tensor_scalar_mul", out=ccol[:, :, 1, :], in0=lam3[6], scalar1=-16.0), [lam_s], [ccol])

    steps = []

    def add(blk, fn):
        steps.append((blk, fn))

    def wview(ap2d):
        return ap2d.rearrange("(kc p) n -> p kc n", p=128)

    def rmsnorm_stats(xt_ap, xreg, T8, ssn=ssn):
        act(OP("activation", out=T8[:], in_=xt_ap, func=AF.Square, accum_out=ssn[:, 0:1]), [xreg], [T8, ssn])
        act(OP("activation", out=ssn[:, 1:2], in_=ssn[:, 0:1], func=AF.Sqrt, scale=1.0 / D, bias=EPS), [ssn], [ssn])
        dve(OP("reciprocal", out=ssn[:, 2:3], in_=ssn[:, 1:2]), [ssn], [ssn])

    def fm_block(wreg, T, src, src_reg, dst_fn, dst_reg, func, bias_fn=None):
        for mi in range(BW // 128):
            ps = small()
            for kc in range(8):
                pe(OP("matmul", ps[:, 0:T], lhsT=wreg[:, kc, mi * 128:(mi + 1) * 128],
                                                          rhs=src[:, kc, :], start=(kc == 0), stop=(kc == 7)),
                   [wreg, src_reg], [ps])
            dst = dst_fn(mi)
            act(OP("activation", out=dst, in_=ps[:, 0:T], func=func), [ps], [dst_reg], partial=True)

    def tm_block(wreg, nt, srcT, src_reg, dst_fn, dst_reg):
        for t in range(nt):
            ps = small()
            for kc in range(8):
                pe(OP("matmul", ps[:, 0:BW], lhsT=srcT[:, kc, t * 128:(t + 1) * 128],
                                                        rhs=wreg[:, kc, :], start=(kc == 0), stop=(kc == 7)),
                   [wreg, src_reg], [ps])
            dst = dst_fn(t)
            act(OP("activation", out=dst, in_=ps[:, 0:BW], func=AF.Copy), [ps], [dst_reg], partial=True)

    def group_layer(g, l, sample, par):
        nt = 1 if sample else NT
        T = nt * 128
        w_in = din["w_in"][l]
        Gx = xbuf(g)
        NSUB = BW // 128
        NBLK = 1024 // BW
        steps = []

        def add(blk, fn, est=8.0):
            steps.append((blk, fn, est))
        dve_steps = set()

        xnT = Ga[:, 0:8 * T].rearrange("p (c t) -> p c t", c=8)
        if sample:
            xe4 = Gb[:, 0:8 * NSB * 11].rearrange("p (c b s) -> p c b s", c=8, b=NSB)
        else:
            xe = Gb[:, 0:8 * (T + 3)].rearrange("p (c t) -> p c t", c=8)
        yT = Gc[:, 0:8 * T].rearrange("p (c t) -> p c t", c=8)
        qk_raw = Gd[:, 0:nt * D].rearrange("p (t d) -> p t d", t=nt)
        v_tok = Ge[:, 0:nt * D].rearrange("p (t d) -> p t d", t=nt)
        zT = Gf[:, 0:8 * T].rearrange("p (c t) -> p c t", c=8)
        gaT = Gg[:, 0:8 * T].rearrange("p (c t) -> p c t", c=8)
        gbT = Gh[:, 0:8 * T].rearrange("p (c t) -> p c t", c=8)

        def s_norm1(_):
            if sample:
                load(rot, rot[:, 0, 0, :], din["rots_c"], partial=True)
                load(rot, rot[:, 1, 0, :], din["rots_s"], partial=True)
            else:
                load(rot, rot[:, 0, :, :], din["rotp_c"][:, g * NT:(g + 1) * NT, :], partial=True)
                load(rot, rot[:, 1, :, :], din["rotp_s"][:, g * NT:(g + 1) * NT, :], partial=True)
            for t in range(nt):
                rmsnorm_stats(Gx[:, t, :], Gx, Tr[6])
                dve(OP("tensor_scalar", out=Tr[7][:], in0=Gx[:, t, :], scalar1=ssn[:, 2:3], scalar2=None, op0=ALU.mult),
                    [Gx, ssn], [Tr[7]])
                ps = big()
                for c in range(8):
                    pe(OP("transpose", out=ps[:, c * 128:(c + 1) * 128], in_=Tr[7][:, c * 128:(c + 1) * 128], identity=ident[:]),
                       [Tr[7], ident], [ps])
                g1 = vecs[:, l, 72:80].unsqueeze(2).to_broadcast([128, 8, 128])
                dve(OP("tensor_tensor", out=xnT[:, :, t * 128:(t + 1) * 128],
                                                                 in0=ps[:].rearrange("p (c t) -> p c t", c=8), in1=g1, op=ALU.mult),
                    [ps, vecs], [Ga], partial=True)
            load(wab, wab[:], din["wab"][l])
            if sample:
                load(h0s, h0s[:], din["srnnT"][l])
                load(cvs, cvs[:], din["sconvT"][l])
                dve(OP("tensor_copy", out=xe4[:, :, :, 0:3], in_=cvs[:]), [cvs], [Gb], partial=True)
            else:
                dve(OP("tensor_copy", out=xe[:, :, 0:3], in_=ctail[:, l, :, :]), [ctail], [Gb], partial=True)
        add(None, s_norm1, 20.0)

        for bi in range(NBLK):
            def s_xr(w, bi=bi):
                if sample:
                    dst = lambda mi: xe4[:, bi * NSUB + mi, :, 3:11]
                    for mi in range(NSUB):
                        ps = small()
                        for kc in range(8):
                            pe(OP("matmul", ps[:, 0:T], lhsT=w[:, kc, mi * 128:(mi + 1) * 128],
                                                                      rhs=xnT[:, kc, :], start=(kc == 0), stop=(kc == 7)), [w, Ga], [ps])
                        act(OP("activation", out=dst(mi), in_=ps[:, 0:T].rearrange("p (b s) -> p b s", b=NSB), func=AF.Copy),
                            [ps], [Gb], partial=True)
                else:
                    fm_block(w, T, xnT, Ga, lambda mi: xe[:, bi * NSUB + mi, 3:3 + T], Gb, AF.Copy)
            add(w_in[:, 3072 + bi * BW:3072 + (bi + 1) * BW], s_xr)
        for bi in range(NBLK):
            add(w_in[:, 4096 + bi * BW:4096 + (bi + 1) * BW],
                lambda w, bi=bi: fm_block(w, T, xnT, Ga, lambda mi: yT[:, bi * NSUB + mi, :], Gc, AF.Gelu_apprx_tanh))

        for bi in range(NBLK):
            add(w_in[:, bi * BW:(bi + 1) * BW],
                lambda w, bi=bi: tm_block(w, nt, xnT, Ga, lambda t: qk_raw[:, t, bi * BW:(bi + 1) * BW], Gd))
        for bi in range(NBLK):
            add(w_in[:, 1024 + bi * BW:1024 + (bi + 1) * BW],
                lambda w, bi=bi: tm_block(w, nt, xnT, Ga, lambda t: v_tok[:, t, bi * BW:(bi + 1) * BW], Ge))
        for bi in range(NBLK):
            add(w_in[:, 2048 + bi * BW:2048 + (bi + 1) * BW],
                lambda w, bi=bi: fm_block(w, T, xnT, Ga, lambda mi: zT[:, bi * NSUB + mi, :], Gf, AF.Silu))
        for bi in range(NBLK):
            add(w_in[:, 5120 + bi * BW:5120 + (bi + 1) * BW],
                lambda w, bi=bi: fm_block(w, T, xnT, Ga, lambda mi: gaT[:, bi * NSUB + mi, :], Gg, AF.Sigmoid))
        for bi in range(NBLK):
            add(w_in[:, 6144 + bi * BW:6144 + (bi + 1) * BW],
                lambda w, bi=bi: fm_block(w, T, xnT, Ga, lambda mi: gbT[:, bi * NSUB + mi, :], Gh, AF.Sigmoid))

        def s_rglru(_):
            xc = xnT
            for c in range(8):
                s = c % 2
                TA, TB = Tr[4 + 2 * s], Tr[5 + 2 * s]
                r_ = TA[:, 0:T]
                i_ = TA[:, 256:256 + T]
                a_ = TA[:, 512:512 + T]
                m_ = TA[:, 768:768 + T]
                u_ = TB[:, 0:T]
                h_ = TB[:, 256:256 + T]
                xc_c = xc[:, c, :]
                if sample:
                    xc_v = xc_c.rearrange("p (b s) -> p b s", b=NSB)
                    win = lambda w: xe4[:, c, :, w:w + ST]
                else:
                    xc_v = xc_c
                    win = lambda w: xe[:, c, w:w + T]
                dve(OP("tensor_scalar", out=xc_v, in0=win(0), scalar1=vcol(l, 0, c * 4), scalar2=vcol(l, 32, c),
                                                                     op0=ALU.mult, op1=ALU.add), [Gb, vecs], [Ga], partial=True)
                for w_ in range(1, 4):
                    dve(OP("scalar_tensor_tensor", out=xc_v, in0=win(w_), scalar=vcol(l, 0, c * 4 + w_), in1=xc_v,
                                                                                      op0=ALU.mult, op1=ALU.add), [Gb, vecs, Ga], [Ga], partial=True)
                psr, psi = small(), small()
                pe(OP("matmul", psr[:, 0:T], lhsT=wab[:, 0, c, :], rhs=xc[:, c, :], start=True, stop=True), [wab, Ga], [psr])
                pe(OP("matmul", psi[:, 0:T], lhsT=wab[:, 1, c, :], rhs=xc[:, c, :], start=True, stop=True), [wab, Ga], [psi])
                act(OP("activation", out=r_, in_=psr[:, 0:T], func=AF.Sigmoid, bias=vcol(l, 40, c)), [psr, vecs], [TA])
                act(OP("activation", out=i_, in_=psi[:, 0:T], func=AF.Sigmoid, bias=vcol(l, 48, c)), [psi, vecs], [TA])
                act(OP("activation", out=a_, in_=r_, func=AF.Exp, scale=ccol[:, l, 0, c:c + 1]), [TA, ccol], [TA])
                act(OP("activation", out=m_, in_=r_, func=AF.Exp, scale=ccol[:, l, 1, c:c + 1]), [TA, ccol], [TA])
                act(OP("activation", out=m_, in_=m_, func=AF.Sqrt, scale=-1.0, bias=1.0), [TA], [TA])
                dve(OP("tensor_mul", out=u_, in0=i_, in1=xc_c), [TA, Ga], [TB])
                dve(OP("tensor_mul", out=u_, in0=u_, in1=m_), [TA, TB], [TB])
                if sample:
                    a3 = a_.rearrange("p (b s) -> p b s", b=NSB)
                    u3 = u_.rearrange("p (b s) -> p b s", b=NSB)
                    h3 = h_.rearrange("p (b s) -> p b s", b=NSB)
                    dve(OP("tensor_mul", out=hls[:, c, :], in0=a3[:, :, 0], in1=h0s[:, c, :]), [TA, h0s], [hls], partial=True)
                    dve(OP("tensor_add", out=u3[:, :, 0], in0=u3[:, :, 0], in1=hls[:, c, :]), [TB, hls], [TB])
                    dve(OP("memset", a3[:, :, 0], 0.0), [], [TA])
                    dve(OP("tensor_tensor_scan", out=h_, data0=a_, data1=u_, initial=0.0, op0=ALU.mult, op1=ALU.add), [TA, TB], [TB])
                    dve(OP("tensor_copy", out=hls[:, c, :], in_=h3[:, :, ST - 1]), [TB], [hls], partial=True)
                else:
                    dve(OP("tensor_tensor_scan", out=h_, data0=a_, data1=u_, initial=hst[:, l, c:c + 1], op0=ALU.mult, op1=ALU.add),
                        [TA, TB, hst], [TB])
                    dve(OP("tensor_copy", out=hst[:, l, c:c + 1], in_=h_[:, T - 1:T]), [TB], [hst])
                dve(OP("tensor_mul", out=yT[:, c, :], in0=yT[:, c, :], in1=h_), [TB, Gc], [Gc], partial=True)
                yield 12.0
            if sample:
                dve(OP("tensor_copy", out=cvs[:], in_=xe4[:, :, :, ST:ST + 3]), [Gb], [cvs])
                store(cvs, dout["scs"][l], cvs[:])
                store(hls, dout["shs"][l], hls[:])
            else:
                dve(OP("tensor_copy", out=ctail[:, l, :, :], in_=xe[:, :, T:T + 3]), [Gb], [ctail])
                if g == 16 // NT - 1:
                    store(ctail, dout["scp"][l], ctail[:, l, :, :])
                    store(hst, dout["shp"][l], hst[:, l, :])
            yield 2.0
        add(None, s_rglru, 98.0)

        def s_ret(_):
            for t in range(nt):
                qkr = Tr[0][:].rearrange("p (h d) -> p h d", h=16)
                qwT = Tr[1][0:64, :].rearrange("p (h j) -> p h j", h=NH)
                kT = Tr[2][0:64, :].rearrange("p (h j) -> p h j", h=NH)
                kwt2 = Tr[3][:].rearrange("p (h d) -> p h d", h=NH)
                kwt = kwt2[:, :, 0:64]
                SmT = Tr[4][:].rearrange("p (h j) -> p h j", h=NH)
                raw = qk_raw[:, t, :].rearrange("p (h d) -> p h d", h=16)
                rc, rs = rot[:, 0, t, :], rot[:, 1, t, :]
                rcr, rsr = rot, rot
                cb_ = rc.unsqueeze(1).to_broadcast([128, 16, 32])
                sb_ = rs.unsqueeze(1).to_broadcast([128, 16, 32])
                x1, x2 = raw[:, :, 0:32], raw[:, :, 32:64]
                o1, o2 = qkr[:, :, 0:32], qkr[:, :, 32:64]
                tmp = Tr[5][:, 0:512].rearrange("p (h d) -> p h d", h=16)
                dve(OP("tensor_tensor", out=o1, in0=x1, in1=cb_, op=ALU.mult), [Gd, rcr], [Tr[0]], partial=True)
                dve(OP("tensor_tensor", out=tmp, in0=x2, in1=sb_, op=ALU.mult), [Gd, rsr], [Tr[5]])
                dve(OP("tensor_tensor", out=o1, in0=o1, in1=tmp, op=ALU.subtract), [Tr[0], Tr[5]], [Tr[0]], partial=True)
                dve(OP("tensor_tensor", out=o2, in0=x1, in1=sb_, op=ALU.mult), [Gd, rsr], [Tr[0]], partial=True)
                dve(OP("tensor_tensor", out=tmp, in0=x2, in1=cb_, op=ALU.mult), [Gd, rcr], [Tr[5]])
                dve(OP("tensor_tensor", out=o2, in0=o2, in1=tmp, op=ALU.add), [Tr[0], Tr[5]], [Tr[0]], partial=True)
                if dbg is not None and dbg.get('sub', 99) < 1:
                    continue
                ps = big()
                for h in range(NH):
                    pe(OP("transpose", out=ps[0:64, h * 128:(h + 1) * 128], in_=qkr[:, h, :], identity=ident[:]), [Tr[0], ident], [ps])
                dve(OP("tensor_tensor", out=qwT, in0=ps[0:64, :].rearrange("p (h j) -> p h j", h=NH), in1=qw[:], op=ALU.mult),
                    [ps, qw], [Tr[1]])
                if dbg is not None and dbg.get('sub', 99) < 2:
                    continue
                ps = big()
                for h in range(NH):
                    pe(OP("transpose", out=ps[0:64, h * 128:(h + 1) * 128], in_=qkr[:, 8 + h, :], identity=ident[:]), [Tr[0], ident], [ps])
                act(OP("activation", out=Tr[2][0:64, :], in_=ps[0:64, :], func=AF.Copy), [ps], [Tr[2]])
                dve(OP("tensor_tensor", out=kwt, in0=qkr[:, 8:16, :], in1=kw[:].unsqueeze(2).to_broadcast([128, NH, 64]), op=ALU.mult),
                    [Tr[0], kw], [Tr[3]])
                if dbg is not None and dbg.get('sub', 99) < 3:
                    continue
                ps = big()
                for h in range(NH):
                    pe(OP("matmul", ps[:, h * 128:(h + 1) * 128], lhsT=kT[:, h, :], rhs=qwT[:, h, :], start=True, stop=True),
                       [Tr[1], Tr[2]], [ps])
                dve(OP("tensor_tensor", out=SmT, in0=ps[:].rearrange("p (h j) -> p h j", h=NH), in1=mask[:], op=ALU.mult),
                    [ps, mask], [Tr[4]])
                if dbg is not None and dbg.get('sub', 99) < 4:
                    continue
                yield 25.0
                pso = big()
                for h in range(NH):
                    pe(OP("matmul", pso[:, h * 128:(h + 1) * 128], lhsT=v_tok[:, t, h * 128:(h + 1) * 128], rhs=SmT[:, h, :],
                                                    start=True, stop=sample), [Ge, Tr[4]], [pso])
                    if not sample:
                        pe(OP("matmul", pso[:, h * 128:(h + 1) * 128], lhsT=Rst[:, l, h, :], rhs=qwT[:, h, :], start=False, stop=True),
                           [Rst, Tr[1]], [pso])
                osb, osq, tm7 = Tr[5], Tr[6], Tr[7]
                act(OP("activation", out=osb[:], in_=pso[:], func=AF.Copy), [pso], [osb])
                if dbg is not None and dbg.get('sub', 99) < 5:
                    continue
                if not sample:
                    psk = big()
                    for h in range(NH):
                        pe(OP("matmul", psk[0:64, h * 128:(h + 1) * 128], lhsT=kwt[:, h, :], rhs=v_tok[:, t, h * 128:(h + 1) * 128],
                                                                 start=True, stop=True), [Tr[3], Ge], [psk])
                    Rl = Rst[:, l, :, :]
                    if dbg is None or dbg.get('sub', 99) >= 7:
                        dve(OP("tensor_tensor", out=Rl, in0=Rl, in1=gc_p[:].unsqueeze(2).to_broadcast([64, NH, 128]), op=ALU.mult), [Rst, gc_p], [Rst])
                    if dbg is None or dbg.get('sub', 99) >= 8:
                        dve(OP("tensor_tensor", out=Rl, in0=psk[0:64, :].rearrange("p (h e) -> p h e", h=NH), in1=Rl, op=ALU.add),
                            [Rst, psk], [Rst])
                    if g == 16 // NT - 1 and t == nt - 1:
                        store(Rst, dout["srp"][l], Rst[:, l, :, :])
                else:
                    R0 = Gb[0:64, 0:NSB * 128].rearrange("p (b e) -> p b e", b=NSB)
                    Vb = Gd[:, 0:NSB * 128].rearrange("p (b e) -> p b e", b=NSB)
                    for h in range(NH):
                        load(Gb, R0, din["sret"][l, h])
                        pc = small()
                        for b in range(NSB):
                            pe(OP("matmul", pc[:, b * ST:(b + 1) * ST], lhsT=R0[:, b, :],
                                                                   rhs=qwT[:, h, b * ST:(b + 1) * ST], start=True, stop=True), [Gb, Tr[1]], [pc])
                        dve(OP("tensor_tensor", out=osb[:, h * 128:(h + 1) * 128], in0=pc[:, 0:128],
                                                                  in1=osb[:, h * 128:(h + 1) * 128], op=ALU.add), [osb, pc], [osb])
                        dve(OP("tensor_tensor", out=Vb, in0=v_tok[:, 0, h * 128:(h + 1) * 128].unsqueeze(1).to_broadcast([128, NSB, 128]),
                                                           in1=oh[:].unsqueeze(2).to_broadcast([128, NSB, 128]), op=ALU.mult), [Ge, oh], [Gd])
                        for q in range(4):
                            pqq = small()
                            pe(OP("matmul", pqq[0:64, :], lhsT=kwt[:, h, :], rhs=Vb[:, q * 4:(q + 1) * 4, :], start=True, stop=True), [Tr[3], Gd], [pqq])
                            dve(OP("tensor_scalar", out=R0[:, q * 4:(q + 1) * 4, :], in0=R0[:, q * 4:(q + 1) * 4, :], scalar1=g8[h], scalar2=None,
                                   op0=ALU.mult), [Gb], [Gb])
                            dve(OP("tensor_tensor", out=R0[:, q * 4:(q + 1) * 4, :], in0=pqq[0:64, :].rearrange("p (b e) -> p b e", b=4),
                                   in1=R0[:, q * 4:(q + 1) * 4, :], op=ALU.add), [Gb, pqq], [Gb])
                        store(Gb, dout["srs"][l, h], R0)
                if dbg is not None and dbg.get('sub', 99) < 9:
                    continue
                yield 25.0
                act(OP("activation", out=osq[:], in_=osb[:], func=AF.Square), [osb], [osq])
                pm, p2 = big(), big()
                for hf in range(2):
                    pe(OP("matmul", pm[:, hf * 512:(hf + 1) * 512], lhsT=onesdiv[:], rhs=osb[:, hf * 512:(hf + 1) * 512], start=True, stop=True),
                       [onesdiv, osb], [pm])
                act(OP("activation", out=tm7[:], in_=pm[:], func=AF.Square), [pm], [tm7])
                if dbg is not None and dbg.get('sub', 99) < 10:
                    continue
                if not (dbg is not None and dbg.get('skipA')):
                    dve(OP("tensor_tensor", out=osb[:], in0=pm[:], in1=osb[:], op=ALU.subtract), [osb, pm], [osb])
                for hf in range(2):
                    pe(OP("matmul", p2[:, hf * 512:(hf + 1) * 512], lhsT=onesdiv[:], rhs=osq[:, hf * 512:(hf + 1) * 512], start=True, stop=True),
                       [onesdiv, osq], [p2])
                if dbg is not None and dbg.get('sub', 99) < 11:
                    continue
                dve(OP("tensor_tensor", out=tm7[:], in0=p2[:], in1=tm7[:], op=ALU.subtract), [p2, tm7], [tm7])
                dve(OP("tensor_scalar_max", out=tm7[:], in0=tm7[:], scalar1=0.0), [tm7], [tm7])
                if dbg is not None and dbg.get('sub', 99) < 12:
                    continue
                act(OP("activation", out=tm7[:], in_=tm7[:], func=AF.Sqrt, bias=EPS), [tm7], [tm7])
                dve(OP("reciprocal", out=tm7[:], in_=tm7[:]), [tm7], [tm7])
                if dbg is not None and dbg.get('sub', 99) < 13:
                    continue
                dve(OP("scalar_tensor_tensor", out=osb[:], in0=osb[:], scalar=-1.0, in1=tm7[:], op0=ALU.mult, op1=ALU.mult), [osb, tm7], [osb])
                if dbg is not None and dbg.get('sub', 99) < 14:
                    continue
                o3 = osb[:].rearrange("p (h j) -> p h j", h=NH)
                dve(OP("tensor_tensor", out=o3, in0=o3, in1=vecs[:, l, 64:72].unsqueeze(2).to_broadcast([128, NH, 128]), op=ALU.mult), [osb, vecs], [osb])
                dve(OP("tensor_tensor", out=zT[:, :, t * 128:(t + 1) * 128], in0=zT[:, :, t * 128:(t + 1) * 128], in1=o3, op=ALU.mult),
                    [Gf, osb], [Gf], partial=True)
                yield 30.0
        add(None, s_ret, 80.0 * nt)

        for bi in range(NBLK):
            def s_ro(w, bi=bi):
                for mi in range(NSUB):
                    ps = small()
                    for kc in range(8):
                        pe(OP("matmul", ps[:, 0:T], lhsT=w[:, kc, mi * 128:(mi + 1) * 128], rhs=zT[:, kc, :],
                                                                  start=(kc == 0), stop=(kc == 7)), [w, Gf], [ps])
                    dst = gaT[:, bi * NSUB + mi, :]
                    dve(OP("tensor_tensor", out=dst, in0=ps[:, 0:T], in1=dst, op=ALU.mult), [ps, Gg], [Gg], partial=True)
            add(din["w_ret"][l][:, bi * BW:(bi + 1) * BW], s_ro)
        for bi in range(NBLK):
            def s_rn(w, bi=bi):
                for mi in range(NSUB):
                    ps = small()
                    for kc in range(8):
                        pe(OP("matmul", ps[:, 0:T], lhsT=w[:, kc, mi * 128:(mi + 1) * 128], rhs=yT[:, kc, :],
                                                                  start=(kc == 0), stop=(kc == 7)), [w, Gc], [ps])
                    dst = gbT[:, bi * NSUB + mi, :]
                    dst2 = gaT[:, bi * NSUB + mi, :]
                    dve(OP("tensor_tensor", out=dst, in0=ps[:, 0:T], in1=dst, op=ALU.mult), [ps, Gh], [Gh], partial=True)
                    dve(OP("tensor_tensor", out=dst2, in0=dst2, in1=dst, op=ALU.add), [Gh, Gg], [Gg], partial=True)
            add(din["w_rnn"][l][:, bi * BW:(bi + 1) * BW], s_rn)
        for bi in range(NBLK):
            def s_wo(w, bi=bi):
                for t in range(nt):
                    ps = small()
                    for kc in range(8):
                        pe(OP("matmul", ps[:, 0:BW], lhsT=gaT[:, kc, t * 128:(t + 1) * 128], rhs=w[:, kc, :],
                                                                start=(kc == 0), stop=(kc == 7)), [w, Gg], [ps])
                    dst = Gx[:, t, bi * BW:(bi + 1) * BW]
                    dve(OP("tensor_tensor", out=dst, in0=ps[:, 0:BW], in1=dst, op=ALU.add), [ps, Gx], [Gx], partial=True)
            add(din["w_o"][l][:, bi * BW:(bi + 1) * BW], s_wo)

        xn2T = Ga[:, 0:8 * T].rearrange("p (c t) -> p c t", c=8)
        xn2 = Ge[:, 0:nt * D].rearrange("p (t d) -> p t d", t=nt)
        qpT = [Gb[:, 0:8 * T].rearrange("p (c t) -> p c t", c=8), Gc[:, 0:8 * T].rearrange("p (c t) -> p c t", c=8)]
        qpR = [Gb, Gc]
        keysT = Gf[:, 0:16 * 128].rearrange("p (a k) -> p a k", a=16)
        sc = Gd[:, 0:16 * 128].rearrange("p (a k) -> p a k", a=16)

        def s_norm2(_):
            for t in range(nt):
                rmsnorm_stats(Gx[:, t, :], Gx, Tr[6])
                dve(OP("scalar_tensor_tensor", out=xn2[:, t, :], in0=Gx[:, t, :], scalar=ssn[:, 2:3], in1=n2g[:, l, :],
                                                         op0=ALU.mult, op1=ALU.mult), [Gx, ssn, n2g], [Ge], partial=True)
                ps = big()
                for c in range(8):
                    pe(OP("transpose", out=ps[:, c * 128:(c + 1) * 128], in_=xn2[:, t, c * 128:(c + 1) * 128], identity=ident[:]),
                       [Ge, ident], [ps])
                act(OP("activation", out=xn2T[:, :, t * 128:(t + 1) * 128], in_=ps[:].rearrange("p (c t) -> p c t", c=8), func=AF.Copy),
                    [ps], [Ga], partial=True)
            load(Gf, keysT, din["keysT"][l])
        add(None, s_norm2, 20.0)
        for bi in range(2048 // BW):
            add(din["wq"][l][:, bi * BW:(bi + 1) * BW],
                lambda w, bi=bi: fm_block(w, T, xn2T, Ga, lambda mi: qpT[(bi * NSUB + mi) // 8][:, (bi * NSUB + mi) % 8, :],
                                          qpR[(bi * NSUB) // 8], AF.Copy))

        def s_peer(_):
            for t in range(nt):
                for half in range(2):
                    ps = big()
                    for a in range(8):
                        pe(OP("matmul", ps[:, a * 128:(a + 1) * 128], lhsT=qpT[half][:, a, t * 128:(t + 1) * 128],
                                                                         rhs=keysT[:, half * 8 + a, :], start=True, stop=True), [qpR[half], Gf], [ps])
                    act(OP("activation", out=Gd[:, half * 1024:(half + 1) * 1024], in_=ps[:], func=AF.Copy), [ps], [Gd], partial=True)
                for a in range(16):
                    for rnd in range(2):
                        sl = slice(rnd * 8, rnd * 8 + 8)
                        dve(OP("max", out=topv[:, a, sl], in_=sc[:, a, :]), [Gd], [topv], partial=True)
                        dve(OP("max_index", out=topi[:, a, sl], in_max=topv[:, a, sl], in_values=sc[:, a, :]), [Gd, topv], [topi], partial=True)
                        if rnd == 0:
                            dve(OP("match_replace", out=sc[:, a, :], in_to_replace=topv[:, a, sl], in_values=sc[:, a, :], imm_value=NEG),
                                [Gd, topv], [Gd])
                dve(OP("tensor_copy", out=topf[:], in_=topi[:]), [topi], [topf])
                yield 40.0
                tv4 = topv[:].rearrange("p (h q) k -> p h q k", q=2)
                tf4 = topf[:].rearrange("p (h q) k -> p h q k", q=2)
                cand = Gg[:, 0:NH * 256].rearrange("p (h c) -> p h c", h=NH)
                cand4 = Gg[:, 0:NH * 256].rearrange("p (h a b) -> p h a b", h=NH, a=16)
                dve(OP("tensor_tensor", out=cand4, in0=tv4[:, :, 0, :].unsqueeze(3).to_broadcast([128, NH, 16, 16]),
                                              in1=tv4[:, :, 1, :].unsqueeze(2).to_broadcast([128, NH, 16, 16]), op=ALU.add), [topv], [Gg])
                for h in range(NH):
                    for rnd in range(2):
                        sl = slice(rnd * 8, rnd * 8 + 8)
                        dve(OP("max", out=b8[:, h, sl], in_=cand[:, h, :]), [Gg], [b8], partial=True)
                        dve(OP("max_index", out=c8[:, h, sl], in_max=b8[:, h, sl], in_values=cand[:, h, :]), [Gg, b8], [c8], partial=True)
                        if rnd == 0:
                            dve(OP("match_replace", out=cand[:, h, :], in_to_replace=b8[:, h, sl], in_values=cand[:, h, :], imm_value=NEG),
                                [Gg, b8], [Gg])
                yield 20.0
                dve(OP("tensor_single_scalar", out=cab[:, 0, :, :], in_=c8[:], scalar=4, op=ALU.logical_shift_right), [c8], [cab], partial=True)
                dve(OP("tensor_single_scalar", out=cab[:, 1, :, :], in_=c8[:], scalar=15, op=ALU.bitwise_and), [c8], [cab], partial=True)
                dve(OP("tensor_copy", out=cabf[:], in_=cab[:]), [cab], [cabf])
                ohA = Gh[:, 0:NH * 256].rearrange("p (h k a) -> p h k a", h=NH, k=16)
                io4 = iota16[:].unsqueeze(1).unsqueeze(1).to_broadcast([128, NH, 16, 16])
                for q in range(2):
                    dve(OP("tensor_tensor", out=ohA, in0=cabf[:, q, :, :].unsqueeze(3).to_broadcast([128, NH, 16, 16]), in1=io4, op=ALU.is_equal),
                        [cabf, iota16], [Gh])
                    dve(OP("tensor_tensor", out=ohA, in0=ohA, in1=tf4[:, :, q, :].unsqueeze(2).to_broadcast([128, NH, 16, 16]), op=ALU.mult),
                        [Gh, topf], [Gh])
                    dve(OP("tensor_reduce", out=isel[:, q, :], in_=Gh[:, 0:NH * 256].rearrange("p (m a) -> p m a", a=16), axis=AX.X, op=ALU.add),
                        [Gh], [isel], partial=True)
                dve(OP("scalar_tensor_tensor", out=isel[:, 2, :], in0=isel[:, 0, :], scalar=128.0, in1=isel[:, 1, :], op0=ALU.mult, op1=ALU.add),
                    [isel], [isel])
                dve(OP("tensor_copy", out=eidxP[par][t][:], in_=isel[:, 2, :]), [isel], [eidxP[par][t]])
                dve(OP("tensor_tensor", out=gsm[:, 0, :, :], in0=b8[:], in1=b8[:, :, 0:1].to_broadcast([128, NH, 16]), op=ALU.subtract), [b8], [gsm], partial=True)
                act(OP("activation", out=gsm[:, 1, :, :], in_=gsm[:, 0, :, :], func=AF.Exp), [gsm], [gsm])
                dve(OP("tensor_reduce", out=gsm[:, 2, :, 0], in_=gsm[:, 1, :, :], axis=AX.X, op=ALU.add), [gsm], [gsm])
                dve(OP("reciprocal", out=gsm[:, 2, :, 1], in_=gsm[:, 2, :, 0]), [gsm], [gsm])
                dve(OP("tensor_tensor", out=gsm[:, 3, :, :], in0=gsm[:, 1, :, :], in1=gsm[:, 2, :, 1:2].to_broadcast([128, NH, 16]), op=ALU.mult), [gsm], [gsm])
                dve(OP("tensor_copy", out=gateP[par][t][:], in_=gsm[:, 3, :, :].rearrange("p h k -> p (h k)")), [gsm], [gateP[par][t]])
                yield 40.0
        add(None, s_peer, 100.0 * nt)

        ms = []

        def gather(tab, t, j):
            gbuf = Gt[j % NG]
            P.op("pool", OP("indirect_dma_start", out=gbuf[:], out_offset=None, in_=tab,
                            in_offset=bass.IndirectOffsetOnAxis(ap=eidxP[par][t][:, j:j + 1], axis=0)),
                 reads=[eidxP[par][t]], writes=[gbuf], dma=gbuf)

        pe_slots = [j for j in range(128) if (j % 2) == 1]
        for t in range(nt):
            def m_prep(t=t):
                rmsnorm_stats(Gx[:, t, :], Gx, Gn, ssn=ssn2)
                dve(OP("scalar_tensor_tensor", out=Gn[:], in0=Gx[:, t, :], scalar=ssn2[:, 2:3], in1=n2g[:, l, :],
                       op0=ALU.mult, op1=ALU.mult), [Gx, ssn2, n2g], [Gn])
                for j in range(NG):
                    gather(pu_d[l], t, j)
            ms.append(m_prep)
            for j in range(128):
                def m_u(t=t, j=j):
                    gbuf = Gt[j % NG]
                    dve(OP("scalar_tensor_tensor", out=gbuf[:], in0=gbuf[:], scalar=1.0, in1=Gn[:], op0=ALU.mult, op1=ALU.mult,
                           accum_out=hpre[:, j:j + 1]), [gbuf, Gn], [gbuf, hpre], partial=True)
                    if j + NG < 128:
                        gather(pu_d[l], t, j + NG)
                ms.append(m_u)

            def m_hid(t=t):
                act(OP("activation", out=hid[:], in_=hpre[:], func=AF.Gelu_apprx_tanh), [hpre], [hid])
                dve(OP("tensor_tensor", out=hid[:], in0=hid[:], in1=gateP[par][t][:], op=ALU.mult), [hid, gateP[par][t]], [hid])
                for j in range(NG):
                    gather(pv_d[l], t, j)
            ms.append(m_hid)
            for j in range(128):
                def m_v(t=t, j=j):
                    gbuf = Gt[j % NG]
                    dve(OP("scalar_tensor_tensor", out=Gx[:, t, :], in0=gbuf[:], scalar=hid[:, j:j + 1], in1=Gx[:, t, :], op0=ALU.mult, op1=ALU.add),
                        [gbuf, hid, Gx], [Gx], partial=True)
                    if j + NG < 128:
                        gather(pv_d[l], t, j + NG)
                ms.append(m_v)
        if l == DEPTH - 1:
            def m_out():
                nf = Gt[NG - 1]
                load(nf, nf[:], din["nfg"].partition_broadcast(128))
                for t in range(nt):
                    yb = Gt[t % 2]
                    rmsnorm_stats(Gx[:, t, :], Gx, Gn, ssn=ssn2)
                    dve(OP("scalar_tensor_tensor", out=yb[:], in0=Gx[:, t, :], scalar=ssn2[:, 2:3], in1=nf[:], op0=ALU.mult, op1=ALU.mult),
                        [Gx, ssn2, nf], [yb])
                    if sample:
                        store(yb, dout["ys"], yb[:])
                    else:
                        r0 = g * TG + t * 128
                        store(yb, dout["yp"][r0:r0 + 128, :], yb[:])
            ms.append(m_out)
        return steps, ms

    ngroups = 16 // NT
    items = []
    for a in range(0, ngroups, 2):
        last = (a + 2 >= ngroups)
        for l in range(DEPTH):
            items.append((a, l, False))
            items.append((a + 1, l, False))
            if last:
                items.append((ngroups, l, True))
    tabs = {"cur": None}

    import types
    wcount = [0]
    pending = []

    def run_s2(n):
        while n > 0 and pending:
            pending.pop(0)()
            n -= 1

    nstep_total = 0
    for k, (g, l, sample) in enumerate(items):
        par = k % 2
        Gx = xbuf(g)
        kind = "s" if sample else "p"
        if tabs["cur"] != kind:
            tabs["cur"] = kind
            load(mask, mask[:], din["mask_" + kind])
            load(qw, qw[:], din["qw_" + kind])
            load(kw, kw[:], din["kw_" + kind])
        if l == 0:
            if sample:
                load(Gx, Gx[:, 0, :], din["xs"])
            else:
                load(Gx, Gx[:], din["xp"][g * TG:(g + 1) * TG, :].rearrange("(t p) d -> p t d", p=128))
        if k > 0 and items[k - 1][0] == g:
            run_s2(len(pending))
        s1, s2 = group_layer(g, l, sample, par)
        tot = sum(e for _, _, e in s1)
        rate = 1.0 / 1.5
        debt = 0.0
        blks = [i for i, (b_, _, _) in enumerate(s1) if b_ is not None]
        loaded = {}
        nxt = 0
        bpos = 0
        stop_all = False
        for i, (blk, fn, est) in enumerate(s1):
            if dbg is not None and nstep_total >= dbg["stop"]:
                stop_all = True
                break
            nstep_total += 1
            want = bpos + (NWB - 1 if blk is not None else NWB - 2)
            while nxt < len(blks) and nxt <= want:
                wreg = wbuf[wcount[0] % NWB]
                wcount[0] += 1
                load(wreg, wreg[:], wview(s1[blks[nxt]][0]))
                loaded[blks[nxt]] = wreg
                nxt += 1
            if blk is not None:
                debt += est * rate
                n = int(debt)
                debt -= n
                run_s2(n)
            hook["on"] = blk is None
            hook["fn"] = run_s2
            r = fn(loaded.get(i))
            if blk is not None:
                bpos += 1
            if isinstance(r, types.GeneratorType):
                for e in r:
                    pass
            hook["on"] = False
        if stop_all:
            break
        run_s2(len(pending))
        pending = list(s2)
    if dbg is None or not stop_all:
        run_s2(len(pending))

    if dbg is not None:
        allregs = {"Gx": GxP[0], "GxB": GxP[1], "Ga": Ga, "Gb": Gb, "Gc": Gc, "Gd": Gd, "Ge": Ge, "Gf": Gf, "Gg": Gg, "Gh": Gh,
                   "Rst": Rst, "hst": hst, "ctail": ctail, "ccol": ccol, "hpre": hpre, "hid": hid, "gsm": gsm,
                   "isel": isel, "topv": topv, "topf": topf, "b8": b8, "cabf": cabf}
        for i in range(8):
            allregs[f"T{i}"] = Tr[i]
        for nm in dbg.get("dump", []):
            reg = allregs[nm]
            shp = list(reg.shape)
            d = nc.dram_tensor("dbg_" + nm, shp, F32, kind="ExternalOutput").ap()
            store(reg, d, reg[:])
    P.finish()
    P.emit()
    P.stack.close()
    return nc


_CACHE = {}


def _f32(a):
    return np.ascontiguousarray(np.asarray(a), dtype=np.float32)


def kernel(x_prompt, x_sample, state_ret, state_rnn, state_conv, norm1_g, norm2_g, normf_g, w_in,
           ret_gn_g, w_ret_out, conv_w, conv_b, rg_wa, rg_ba, rg_wx, rg_bx, rg_lambda, w_rnn_out, w_o,
           peer_wq, peer_keys, peer_u, peer_v):
    consts, g8 = host_consts()
    if "nc" not in _CACHE:
        _CACHE["nc"] = build_program(g8)
    nc = _CACHE["nc"]

    x_prompt = _f32(x_prompt); x_sample = _f32(x_sample)
    state_ret = _f32(state_ret); state_rnn = _f32(state_rnn); state_conv = _f32(state_conv)
    peer_u = _f32(peer_u); peer_v = _f32(peer_v)

    def fm(v):
        return _f32(v).reshape(DEPTH, 8, 128).transpose(0, 2, 1)

    vecs = np.zeros((DEPTH, 128, NV), np.float32)
    vecs[:, :, 0:32] = _f32(conv_w).reshape(DEPTH, 4, 8, 128).transpose(0, 3, 2, 1).reshape(DEPTH, 128, 32)
    vecs[:, :, 32:40] = fm(conv_b)
    vecs[:, :, 40:48] = fm(rg_ba)
    vecs[:, :, 48:56] = fm(rg_bx)
    vecs[:, :, 56:64] = fm(rg_lambda)
    vecs[:, :, 64:72] = fm(ret_gn_g)
    vecs[:, :, 72:80] = fm(norm1_g)
    wab = np.zeros((DEPTH, 128, 2, 8, 128), np.float32)
    for qi, wsrc in enumerate((_f32(rg_wa), _f32(rg_wx))):
        for nl in range(2):
            blk = wsrc[:, nl::2]
            wab[:, nl * 64:(nl + 1) * 64, qi, :, nl * 64:(nl + 1) * 64] = blk.transpose(0, 2, 1, 3)
    keysT = np.ascontiguousarray(_f32(peer_keys).reshape(DEPTH, 16, 128, 128).transpose(0, 3, 1, 2))
    shared = {
        "w_in": _f32(w_in), "w_ret": _f32(w_ret_out), "w_rnn": _f32(w_rnn_out), "w_o": _f32(w_o),
        "wq": _f32(peer_wq), "keysT": keysT,
        "pu0": peer_u[0], "pu1": peer_u[1], "pv0": peer_v[0], "pv1": peer_v[1],
        "vecs": vecs, "n2g": _f32(norm2_g), "nfg": _f32(normf_g), "wab": wab,
    }
    shared.update(consts)
    in_maps = []
    for c in range(NCORES):
        sl = slice(c * NSB, (c + 1) * NSB)
        m = dict(shared)
        m["xp"] = x_prompt[c]
        m["xs"] = np.ascontiguousarray(x_sample[sl].reshape(128, D))
        m["sret"] = np.ascontiguousarray(state_ret[:, sl].transpose(0, 2, 3, 1, 4))
        m["sconvT"] = np.ascontiguousarray(state_conv[:, sl].reshape(DEPTH, NSB, 3, 8, 128).transpose(0, 4, 3, 1, 2))
        m["srnnT"] = np.ascontiguousarray(state_rnn[:, sl].reshape(DEPTH, NSB, 8, 128).transpose(0, 3, 2, 1))
        in_maps.append(m)
    res = run_bass_kernel_spmd(nc, in_maps, core_ids=list(range(NCORES)))
    outs = res.results

    y_p = np.stack([outs[c]["yp"] for c in range(NCORES)], 0)
    y_s = np.concatenate([outs[c]["ys"].reshape(NSB, ST, D) for c in range(NCORES)], 0)
    sr_p = np.stack([outs[c]["srp"].transpose(0, 2, 1, 3) for c in range(NCORES)], 1)
    sh_p = np.stack([outs[c]["shp"].transpose(0, 2, 1).reshape(DEPTH, D) for c in range(NCORES)], 1)
    sc_p = np.stack([outs[c]["scp"].transpose(0, 3, 2, 1).reshape(DEPTH, 3, D) for c in range(NCORES)], 1)
    sr_s = np.concatenate([outs[c]["srs"].transpose(0, 3, 1, 2, 4) for c in range(NCORES)], 1)
    sh_s = np.concatenate([outs[c]["shs"].transpose(0, 3, 2, 1).reshape(DEPTH, NSB, D) for c in range(NCORES)], 1)
    sc_s = np.concatenate([outs[c]["scs"].transpose(0, 3, 4, 2, 1).reshape(DEPTH, NSB, 3, D) for c in range(NCORES)], 1)
    f = lambda a: np.ascontiguousarray(a, dtype=np.float32)
    return (f(y_p), f(y_s), f(sr_p), f(sh_p), f(sc_p), f(sr_s), f(sh_s), f(sc_s))
```

```python
import math
from contextlib import ExitStack
import numpy as np
import concourse.bass as bass
import concourse.mybir as mybir
from concourse.bass_utils import run_bass_kernel_spmd

F32 = mybir.dt.float32
I32 = mybir.dt.int32
U32 = mybir.dt.uint32
AF = mybir.ActivationFunctionType
ALU = mybir.AluOpType
AX = mybir.AxisListType

NCORES = 8
D = 1024
SEQ = 2048
DEPTH = 2
NSB = 16
ST = 8
PAST = 16384
NH = 8
DK = 64
DV = 128
EPS = 1e-6
NEXP = 16384
NT = 2
NG = 6
BW = 256
NV = 80
NEG = -1.0e30


class MonoSem:
    PERIOD = 24000

    def __init__(self, prog, name):
        self.prog = prog
        self.name = name
        self.handles = {}
        self.count = 0

    def handle(self, idx):
        if idx not in self.handles:
            self.handles[idx] = self.prog.stack.enter_context(
                self.prog.nc.semaphore(f"{self.name}_{idx}"))
        return self.handles[idx]

    def next_inc(self, step):
        P = self.PERIOD
        if (self.count % P) + step > P:
            self.count = (self.count // P + 1) * P
        idx = self.count // P
        self.count += step
        return self.handle(idx), self.count

    def target(self, value):
        P = self.PERIOD
        idx = (value - 1) // P
        return self.handle(idx), value - idx * P


class Buf:
    def __init__(self, prog, name):
        self.prog = prog
        self.name = name
        self.w = None
        self.r = {}
        self.excl = False
        self._dsem = {}

    def dsem(self, eng):
        if eng not in self._dsem:
            self._dsem[eng] = MonoSem(self.prog, "d" + eng + "_" + self.name)
        return self._dsem[eng]


class Reg:
    def __init__(self, prog, name, shape, dtype=F32, psum=False):
        self.t = (prog.psum if psum else prog.sbuf)(name, shape, dtype)
        self.b = prog.buf(name)
        self.b.excl = psum
        self.shape = shape

    def __getitem__(self, k):
        return self.t[k]


def OP(name, *a, **k):
    return (name, a, k)


class Prog:
    ENG = ("pe", "act", "dve", "pool", "sp")

    def __init__(self, nc):
        self.nc = nc
        self.stack = ExitStack()
        self.esem = {e: MonoSem(self, "e_" + e) for e in self.ENG}
        self.seen = {e: {} for e in self.ENG}
        self.ops = {e: [] for e in self.ENG}
        self.nbuf = 0
        self.final_events = []

    def buf(self, name=None):
        self.nbuf += 1
        return Buf(self, name or f"b{self.nbuf}")

    def sbuf(self, name, shape, dtype=F32):
        return self.stack.enter_context(self.nc.sbuf_tensor("sb_" + name, list(shape), dtype))

    def psum(self, name, shape, dtype=F32):
        return self.stack.enter_context(self.nc.psum_tensor("ps_" + name, list(shape), dtype))

    def _need(self, eng, ev, waits):
        if ev is None:
            return
        sem, val = ev
        if self.seen[eng].get(sem, 0) >= val:
            return
        self.seen[eng][sem] = val
        waits.append((sem, val))

    def op(self, eng, fn, reads=(), writes=(), dma=None, partial=False, final=False):
        reads = [getattr(b, "b", b) for b in reads]
        writes = [getattr(b, "b", b) for b in writes]
        dma = getattr(dma, "b", dma)
        waits = []
        mysem = dma.dsem(eng) if dma is not None else self.esem[eng]
        for b in reads:
            self._need(eng, b.w, waits)
            if b.excl:
                for sem, val in b.r.items():
                    if sem is not mysem:
                        self._need(eng, (sem, val), waits)
        for b in writes:
            if b.w is not None and not (partial and b.w[0] is mysem):
                self._need(eng, b.w, waits)
            for sem, val in b.r.items():
                self._need(eng, (sem, val), waits)
        step = 16 if dma is not None else 1
        h, val = mysem.next_inc(step)
        ev = (mysem, val)
        for b in reads:
            if b.r.get(mysem, 0) < val:
                b.r[mysem] = val
        for b in writes:
            b.w = ev
            if not partial:
                b.r = {}
        if final:
            self.final_events.append(ev)
        self.ops[eng].append((waits, fn, h, step))
        return ev

    def finish(self):
        waits = []
        for ev in self.final_events:
            self._need("sp", ev, waits)
        self.ops["sp"].append((waits, None, None, 0))

    def emit(self):
        prog = self

        def replay(ename):
            def run(eng):
                for waits, fn, h, step in prog.ops[ename]:
                    for sem, val in waits:
                        sh, local = sem.target(val)
                        eng.wait_ge(sh, local)
                    if fn is None:
                        continue
                    name, a, k = fn
                    getattr(eng, name)(*a, **k).then_inc(h, step)
            return run

        with self.nc.Block() as block:
            block.sync(replay("sp"))
            block.scalar(replay("act"))
            block.vector(replay("dve"))
            block.gpsimd(replay("pool"))
            block.tensor(replay("pe"))


def host_consts():
    logg = np.log1p(-(2.0 ** (-5.0 - np.arange(NH, dtype=np.float64))))
    freq = 10000.0 ** (-np.arange(32, dtype=np.float64) / 32.0)
    c = {}
    c["ident"] = np.eye(128)
    c["onesdiv"] = np.full((128, 128), 1.0 / 128.0)
    ang = np.arange(SEQ, dtype=np.float64)[:, None] * freq[None, :]
    c["rotp_c"] = np.cos(ang).reshape(16, 128, 32).transpose(1, 0, 2)
    c["rotp_s"] = np.sin(ang).reshape(16, 128, 32).transpose(1, 0, 2)
    tm = np.arange(128) % ST
    bm = np.arange(128) // ST
    angs = (PAST + tm).astype(np.float64)[:, None] * freq[None, :]
    c["rots_c"] = np.cos(angs)
    c["rots_s"] = np.sin(angs)
    m = np.arange(128)
    j = np.arange(128)
    mk = 0.125 * np.exp(-(m[:, None, None] + 1.0) * logg[None, :, None]) * (j[None, None, :] >= m[:, None, None])
    c["mask_p"] = mk
    c["qw_p"] = np.broadcast_to(np.exp((j[None, None, :] + 1.0) * logg[None, :, None]), (64, NH, 128))
    c["kw_p"] = 0.125 * np.exp((127.0 - m[:, None]) * logg[None, :])
    c["gc_p"] = np.broadcast_to(np.exp(128.0 * logg)[None, :], (64, NH))
    same = (bm[:, None] == bm[None, :]) & (tm[None, :] >= tm[:, None])
    mks = 0.125 * np.exp(-(tm[:, None, None] + 1.0) * logg[None, :, None]) * same[:, None, :]
    c["mask_s"] = mks
    c["qw_s"] = np.broadcast_to(np.exp((tm[None, None, :] + 1.0) * logg[None, :, None]), (64, NH, 128))
    c["kw_s"] = 0.125 * np.exp((ST - 1.0 - tm[:, None]) * logg[None, :])
    c["oh"] = (bm[:, None] == np.arange(NSB)[None, :]).astype(np.float64)
    c["iota16"] = np.broadcast_to(np.arange(16, dtype=np.float64)[None, :], (128, 16))
    g8 = [float(np.exp(ST * logg[h])) for h in range(NH)]
    return {k: np.ascontiguousarray(v, dtype=np.float32) for k, v in c.items()}, g8


CONST_SHAPES = {
    "ident": [128, 128], "onesdiv": [128, 128], "rotp_c": [128, 16, 32], "rotp_s": [128, 16, 32],
    "rots_c": [128, 32], "rots_s": [128, 32], "mask_p": [128, NH, 128], "qw_p": [64, NH, 128],
    "kw_p": [128, NH], "gc_p": [64, NH], "mask_s": [128, NH, 128], "qw_s": [64, NH, 128],
    "kw_s": [128, NH], "oh": [128, NSB], "iota16": [128, 16],
}

IN_SHAPES = {
    "xp": [SEQ, D], "xs": [128, D],
    "w_in": [DEPTH, D, 7168], "w_ret": [DEPTH, D, D], "w_rnn": [DEPTH, D, D], "w_o": [DEPTH, D, D],
    "wq": [DEPTH, D, 2048], "keysT": [DEPTH, 128, 16, 128],
    "pu0": [NEXP, D], "pu1": [NEXP, D], "pv0": [NEXP, D], "pv1": [NEXP, D],
    "vecs": [DEPTH, 128, NV], "n2g": [DEPTH, D], "nfg": [D],
    "wab": [DEPTH, 128, 2, 8, 128],
    "sret": [DEPTH, NH, 64, NSB, 128], "sconvT": [DEPTH, 128, 8, NSB, 3], "srnnT": [DEPTH, 128, 8, NSB],
}
OUT_SHAPES = {
    "yp": [SEQ, D], "ys": [128, D],
    "srp": [DEPTH, 64, NH, 128], "shp": [DEPTH, 128, 8], "scp": [DEPTH, 128, 8, 3],
    "srs": [DEPTH, NH, 64, NSB, 128], "shs": [DEPTH, 128, 8, NSB], "scs": [DEPTH, 128, 8, NSB, 3],
}


def build_program(g8, dbg=None):
    nc = bass.Bass("TRN2", target_bir_lowering=False)
    P = Prog(nc)
    din = {}
    for k, s in {**IN_SHAPES, **CONST_SHAPES}.items():
        if dbg is not None and dbg.get("small") and k in ("pu0", "pu1", "pv0", "pv1"):
            s = [128, D]
        din[k] = nc.dram_tensor(k, list(s), F32, kind="ExternalInput").ap()
    dout = {k: nc.dram_tensor(k, list(s), F32, kind="ExternalOutput").ap() for k, s in OUT_SHAPES.items()}
    pu_d = [din["pu0"], din["pu1"]]
    pv_d = [din["pv0"], din["pv1"]]

    TG = NT * 128
    GC = 8 * TG

    def R(name, shape, dtype=F32):
        return Reg(P, name, shape, dtype)

    ident = R("ident", [128, 128])
    onesdiv = R("onesdiv", [128, 128])
    rot = R("rot", [128, 2, NT, 32])
    mask = R("mask", [128, NH, 128])
    qw = R("qw", [64, NH, 128])
    kw = R("kw", [128, NH])
    gc_p = R("gc_p", [64, NH])
    oh = R("oh", [128, NSB])
    iota16 = R("iota16", [128, 16])
    vecs = R("vecs", [128, DEPTH, NV])
    ccol = R("ccol", [128, DEPTH, 2, 8])
    n2g = R("n2g", [128, DEPTH, D])
    wab = R("wab", [128, 2, 8, 128])
    Rst = R("Rst", [64, DEPTH, NH, 128])
    hst = R("hst", [128, DEPTH, 8])
    ctail = R("ctail", [128, DEPTH, 8, 3])
    ssn = R("ssn", [128, 4])
    ssn2 = R("ssn2", [128, 4])
    topv = R("topv", [128, 16, 16])
    topi = R("topi", [128, 16, 16], U32)
    topf = R("topf", [128, 16, 16])
    b8 = R("b8", [128, NH, 16])
    c8 = R("c8", [128, NH, 16], U32)
    cab = R("cab", [128, 2, NH, 16], U32)
    cabf = R("cabf", [128, 2, NH, 16])
    isel = R("isel", [128, 3, NH * 16])
    eidxP = [[R(f"eidx{a}{b}", [128, 128], I32) for b in range(NT)] for a in range(2)]
    gateP = [[R(f"gate{a}{b}", [128, 128]) for b in range(NT)] for a in range(2)]
    gsm = R("gsm", [128, 4, NH, 16])
    hpre = R("hpre", [128, 128])
    hid = R("hid", [128, 128])
    GxP = [R("GxA", [128, NT, D]), R("GxB", [128, NT, D])]
    Gn = R("Gn", [128, D])
    Gt = [R(f"Gt{i}", [128, D]) for i in range(NG)]
    Ga = R("Ga", [128, GC])
    Gb = R("Gb", [128, 8 * (TG + 3)])
    Gc = R("Gc", [128, GC])
    Gd = R("Gd", [128, GC])
    Ge = R("Ge", [128, GC])
    Gf = R("Gf", [128, GC])
    Gg = R("Gg", [128, GC])
    Gh = R("Gh", [128, GC])
    Tr = [R(f"T{i}", [128, 1024]) for i in range(8)]

    class _View:
        def __init__(self, reg, ap, shape):
            self.t = ap
            self.b = reg.b
            self.shape = shape

        def __getitem__(self, k):
            return self.t[k]
    lam_s = _View(Tr[0], Tr[0][:, 0:8 * DEPTH * 8].rearrange("p (a b) -> p a b", a=8), [128, 8, DEPTH * 8])
    h0s = _View(Gh, Gh[:, 1024:1024 + 8 * NSB].rearrange("p (c b) -> p c b", c=8), [128, 8, NSB])
    hls = _View(Gh, Gh[:, 1280:1280 + 8 * NSB].rearrange("p (c b) -> p c b", c=8), [128, 8, NSB])
    cvs = _View(Gh, Gh[:, 1536:1536 + 8 * NSB * 3].rearrange("p (c b w) -> p c b w", c=8, b=NSB), [128, 8, NSB, 3])
    NWB = 2
    wbuf = [R(f"wb{i}", [128, 8, BW]) for i in range(NWB)]
    psb = [Reg(P, f"pb{i}", [128, 1024], psum=True) for i in range(2)]
    pss = [Reg(P, f"pq{i}", [128, 512], psum=True) for i in range(4)]
    cnt = {"pb": 0, "pq": 0, "wb": 0}

    def big():
        cnt["pb"] += 1
        return psb[cnt["pb"] % 2]

    def small():
        cnt["pq"] += 1
        return pss[cnt["pq"] % 4]

    hook = {"on": False, "credit": 0.0, "rate": 0.7, "fn": None, "busy": False}

    def dve(fn, r, w, **k):
        if hook["on"] and not hook["busy"]:
            hook["credit"] += hook["rate"]
            while hook["credit"] >= 1.0:
                hook["credit"] -= 1.0
                hook["busy"] = True
                hook["fn"](1)
                hook["busy"] = False
        P.op("dve", fn, reads=r, writes=w, **k)

    def act(fn, r, w, **k):
        P.op("act", fn, reads=r, writes=w, **k)

    def pe(fn, r, w, **k):
        P.op("pe", fn, reads=r, writes=w, partial=True, **k)

    def pool(fn, r, w, **k):
        P.op("pool", fn, reads=r, writes=w, **k)

    def load(reg, out_ap, in_ap, partial=False):
        P.op("sp", OP("dma_start", out=out_ap, in_=in_ap), writes=[reg], dma=reg, partial=partial)

    def store(reg, out_ap, in_ap):
        P.op("sp", OP("dma_start", out=out_ap, in_=in_ap), reads=[reg], dma=reg, final=True)

    for nm, reg in [("ident", ident), ("onesdiv", onesdiv), ("gc_p", gc_p), ("oh", oh), ("iota16", iota16)]:
        load(reg, reg[:], din[nm])
    load(vecs, vecs[:], din["vecs"].rearrange("l p n -> p l n"))
    for l in range(DEPTH):
        load(n2g, n2g[:, l, :], din["n2g"][l].partition_broadcast(128), partial=True)
    pool(OP("memset", Rst[:], 0.0), [], [Rst])
    pool(OP("memset", hst[:], 0.0), [], [hst])
    pool(OP("memset", ctail[:], 0.0), [], [ctail])

    def vcol(l, base, c):
        return vecs[:, l, base + c:base + c + 1]

    LW = DEPTH * 8

    def ls(i):
        return lam_s[:, i, :]

    lam_v = vecs[:, :, 56:64]
    lam3 = [lam_s[:, i, :].rearrange("p (l c) -> p l c", l=DEPTH) for i in range(8)]
    dve(OP("tensor_scalar_mul", out=lam3[0], in0=lam_v, scalar1=-1.0), [vecs], [lam_s])
    dve(OP("tensor_max", out=lam3[0], in0=lam3[0], in1=lam_v), [vecs, lam_s], [lam_s])
    act(OP("activation", out=ls(1), in_=ls(0), func=AF.Exp, scale=-1.0), [lam_s], [lam_s])
    dve(OP("tensor_scalar_add", out=ls(2), in0=ls(1), scalar1=2.0), [lam_s], [lam_s])
    dve(OP("reciprocal", out=ls(2), in_=ls(2)), [lam_s], [lam_s])
    dve(OP("tensor_mul", out=ls(2), in0=ls(2), in1=ls(1)), [lam_s], [lam_s])
    dve(OP("tensor_mul", out=ls(3), in0=ls(2), in1=ls(2)), [lam_s], [lam_s])
    dve(OP("tensor_scalar", out=ls(4), in0=ls(3), scalar1=1.0 / 11.0, scalar2=1.0 / 9.0, op0=ALU.mult, op1=ALU.add), [lam_s], [lam_s])
    for coef in (1.0 / 7.0, 1.0 / 5.0, 1.0 / 3.0, 1.0):
        dve(OP("tensor_mul", out=ls(4), in0=ls(4), in1=ls(3)), [lam_s], [lam_s])
        dve(OP("tensor_scalar_add", out=ls(4), in0=ls(4), scalar1=coef), [lam_s], [lam_s])
    dve(OP("tensor_mul", out=ls(4), in0=ls(4), in1=ls(2)), [lam_s], [lam_s])
    dve(OP("tensor_scalar", out=lam3[5], in0=lam_v, scalar1=-1.0, scalar2=0.0, op0=ALU.mult, op1=ALU.max), [vecs], [lam_s])
    dve(OP("scalar_tensor_tensor", out=ls(6), in0=ls(4), scalar=2.0, in1=ls(5), op0=ALU.mult, op1=ALU.add), [lam_s], [lam_s])
    dve(OP("tensor_scalar_mul", out=ccol[:, :, 0, :], in0=lam3[6], scalar1=-8.0), [lam_s], [ccol])
    dve(OP("tensor_scalar_mul", out=ccol[:, :, 1, :], in0=lam3[6], scalar1=-16.0), [lam_s], [ccol])

    steps = []

    def add(blk, fn):
        steps.append((blk, fn))

    def wview(ap2d):
        return ap2d.rearrange("(kc p) n -> p kc n", p=128)

    def rmsnorm_stats(xt_ap, xreg, T8, ssn=ssn):
        act(OP("activation", out=T8[:], in_=xt_ap, func=AF.Square, accum_out=ssn[:, 0:1]), [xreg], [T8, ssn])
        act(OP("activation", out=ssn[:, 1:2], in_=ssn[:, 0:1], func=AF.Sqrt, scale=1.0 / D, bias=EPS), [ssn], [ssn])
        dve(OP("reciprocal", out=ssn[:, 2:3], in_=ssn[:, 1:2]), [ssn], [ssn])

    def fm_block(wreg, T, src, src_reg, dst_fn, dst_reg, func, bias_fn=None):
        for mi in range(BW // 128):
            ps = small()
            for kc in range(8):
                pe(OP("matmul", ps[:, 0:T], lhsT=wreg[:, kc, mi * 128:(mi + 1) * 128],
                                                          rhs=src[:, kc, :], start=(kc == 0), stop=(kc == 7)),
                   [wreg, src_reg], [ps])
            dst = dst_fn(mi)
            act(OP("activation", out=dst, in_=ps[:, 0:T], func=func), [ps], [dst_reg], partial=True)

    def tm_block(wreg, nt, srcT, src_reg, dst_fn, dst_reg):
        for t in range(nt):
            ps = small()
            for kc in range(8):
                pe(OP("matmul", ps[:, 0:BW], lhsT=srcT[:, kc, t * 128:(t + 1) * 128],
                                                        rhs=wreg[:, kc, :], start=(kc == 0), stop=(kc == 7)),
                   [wreg, src_reg], [ps])
            dst = dst_fn(t)
            act(OP("activation", out=dst, in_=ps[:, 0:BW], func=AF.Copy), [ps], [dst_reg], partial=True)

    def group_layer(g, l, sample, par):
        nt = 1 if sample else NT
        T = nt * 128
        w_in = din["w_in"][l]
        Gx = GxP[g % 2]
        NSUB = BW // 128
        NBLK = 1024 // BW
        steps = []

        def add(blk, fn, est=8.0):
            steps.append((blk, fn, est))
        dve_steps = set()

        xnT = Ga[:, 0:8 * T].rearrange("p (c t) -> p c t", c=8)
        if sample:
            xe4 = Gb[:, 0:8 * NSB * 11].rearrange("p (c b s) -> p c b s", c=8, b=NSB)
        else:
            xe = Gb[:, 0:8 * (T + 3)].rearrange("p (c t) -> p c t", c=8)
        yT = Gc[:, 0:8 * T].rearrange("p (c t) -> p c t", c=8)
        qk_raw = Gd[:, 0:nt * D].rearrange("p (t d) -> p t d", t=nt)
        v_tok = Ge[:, 0:nt * D].rearrange("p (t d) -> p t d", t=nt)
        zT = Gf[:, 0:8 * T].rearrange("p (c t) -> p c t", c=8)
        gaT = Gg[:, 0:8 * T].rearrange("p (c t) -> p c t", c=8)
        gbT = Gh[:, 0:8 * T].rearrange("p (c t) -> p c t", c=8)

        def s_norm1(_):
            if sample:
                load(rot, rot[:, 0, 0, :], din["rots_c"], partial=True)
                load(rot, rot[:, 1, 0, :], din["rots_s"], partial=True)
            else:
                load(rot, rot[:, 0, :, :], din["rotp_c"][:, g * NT:(g + 1) * NT, :], partial=True)
                load(rot, rot[:, 1, :, :], din["rotp_s"][:, g * NT:(g + 1) * NT, :], partial=True)
            for t in range(nt):
                rmsnorm_stats(Gx[:, t, :], Gx, Tr[6])
                dve(OP("tensor_scalar", out=Tr[7][:], in0=Gx[:, t, :], scalar1=ssn[:, 2:3], scalar2=None, op0=ALU.mult),
                    [Gx, ssn], [Tr[7]])
                ps = big()
                for c in range(8):
                    pe(OP("transpose", out=ps[:, c * 128:(c + 1) * 128], in_=Tr[7][:, c * 128:(c + 1) * 128], identity=ident[:]),
                       [Tr[7], ident], [ps])
                g1 = vecs[:, l, 72:80].unsqueeze(2).to_broadcast([128, 8, 128])
                dve(OP("tensor_tensor", out=xnT[:, :, t * 128:(t + 1) * 128],
                                                                 in0=ps[:].rearrange("p (c t) -> p c t", c=8), in1=g1, op=ALU.mult),
                    [ps, vecs], [Ga], partial=True)
            load(wab, wab[:], din["wab"][l])
            if sample:
                load(h0s, h0s[:], din["srnnT"][l])
                load(cvs, cvs[:], din["sconvT"][l])
                dve(OP("tensor_copy", out=xe4[:, :, :, 0:3], in_=cvs[:]), [cvs], [Gb], partial=True)
            else:
                dve(OP("tensor_copy", out=xe[:, :, 0:3], in_=ctail[:, l, :, :]), [ctail], [Gb], partial=True)
        add(None, s_norm1, 20.0)

        for bi in range(NBLK):
            def s_xr(w, bi=bi):
                if sample:
                    dst = lambda mi: xe4[:, bi * NSUB + mi, :, 3:11]
                    for mi in range(NSUB):
                        ps = small()
                        for kc in range(8):
                            pe(OP("matmul", ps[:, 0:T], lhsT=w[:, kc, mi * 128:(mi + 1) * 128],
                                                                      rhs=xnT[:, kc, :], start=(kc == 0), stop=(kc == 7)), [w, Ga], [ps])
                        act(OP("activation", out=dst(mi), in_=ps[:, 0:T].rearrange("p (b s) -> p b s", b=NSB), func=AF.Copy),
                            [ps], [Gb], partial=True)
                else:
                    fm_block(w, T, xnT, Ga, lambda mi: xe[:, bi * NSUB + mi, 3:3 + T], Gb, AF.Copy)
            add(w_in[:, 3072 + bi * BW:3072 + (bi + 1) * BW], s_xr)
        for bi in range(NBLK):
            add(w_in[:, 4096 + bi * BW:4096 + (bi + 1) * BW],
                lambda w, bi=bi: fm_block(w, T, xnT, Ga, lambda mi: yT[:, bi * NSUB + mi, :], Gc, AF.Gelu_apprx_tanh))

        for bi in range(NBLK):
            add(w_in[:, bi * BW:(bi + 1) * BW],
                lambda w, bi=bi: tm_block(w, nt, xnT, Ga, lambda t: qk_raw[:, t, bi * BW:(bi + 1) * BW], Gd))
        for bi in range(NBLK):
            add(w_in[:, 1024 + bi * BW:1024 + (bi + 1) * BW],
                lambda w, bi=bi: tm_block(w, nt, xnT, Ga, lambda t: v_tok[:, t, bi * BW:(bi + 1) * BW], Ge))
        for bi in range(NBLK):
            add(w_in[:, 2048 + bi * BW:2048 + (bi + 1) * BW],
                lambda w, bi=bi: fm_block(w, T, xnT, Ga, lambda mi: zT[:, bi * NSUB + mi, :], Gf, AF.Silu))
        for bi in range(NBLK):
            add(w_in[:, 5120 + bi * BW:5120 + (bi + 1) * BW],
                lambda w, bi=bi: fm_block(w, T, xnT, Ga, lambda mi: gaT[:, bi * NSUB + mi, :], Gg, AF.Sigmoid))
        for bi in range(NBLK):
            add(w_in[:, 6144 + bi * BW:6144 + (bi + 1) * BW],
                lambda w, bi=bi: fm_block(w, T, xnT, Ga, lambda mi: gbT[:, bi * NSUB + mi, :], Gh, AF.Sigmoid))

        def s_rglru(_):
            xc = xnT
            for c in range(8):
                s = c % 2
                TA, TB = Tr[4 + 2 * s], Tr[5 + 2 * s]
                r_ = TA[:, 0:T]
                i_ = TA[:, 256:256 + T]
                a_ = TA[:, 512:512 + T]
                m_ = TA[:, 768:768 + T]
                u_ = TB[:, 0:T]
                h_ = TB[:, 256:256 + T]
                xc_c = xc[:, c, :]
                if sample:
                    xc_v = xc_c.rearrange("p (b s) -> p b s", b=NSB)
                    win = lambda w: xe4[:, c, :, w:w + ST]
                else:
                    xc_v = xc_c
                    win = lambda w: xe[:, c, w:w + T]
                dve(OP("tensor_scalar", out=xc_v, in0=win(0), scalar1=vcol(l, 0, c * 4), scalar2=vcol(l, 32, c),
                                                                     op0=ALU.mult, op1=ALU.add), [Gb, vecs], [Ga], partial=True)
                for w_ in range(1, 4):
                    dve(OP("scalar_tensor_tensor", out=xc_v, in0=win(w_), scalar=vcol(l, 0, c * 4 + w_), in1=xc_v,
                                                                                      op0=ALU.mult, op1=ALU.add), [Gb, vecs, Ga], [Ga], partial=True)
                psr, psi = small(), small()
                pe(OP("matmul", psr[:, 0:T], lhsT=wab[:, 0, c, :], rhs=xc[:, c, :], start=True, stop=True), [wab, Ga], [psr])
                pe(OP("matmul", psi[:, 0:T], lhsT=wab[:, 1, c, :], rhs=xc[:, c, :], start=True, stop=True), [wab, Ga], [psi])
                act(OP("activation", out=r_, in_=psr[:, 0:T], func=AF.Sigmoid, bias=vcol(l, 40, c)), [psr, vecs], [TA])
                act(OP("activation", out=i_, in_=psi[:, 0:T], func=AF.Sigmoid, bias=vcol(l, 48, c)), [psi, vecs], [TA])
                act(OP("activation", out=a_, in_=r_, func=AF.Exp, scale=ccol[:, l, 0, c:c + 1]), [TA, ccol], [TA])
                act(OP("activation", out=m_, in_=r_, func=AF.Exp, scale=ccol[:, l, 1, c:c + 1]), [TA, ccol], [TA])
                act(OP("activation", out=m_, in_=m_, func=AF.Sqrt, scale=-1.0, bias=1.0), [TA], [TA])
                dve(OP("tensor_mul", out=u_, in0=i_, in1=xc_c), [TA, Ga], [TB])
                dve(OP("tensor_mul", out=u_, in0=u_, in1=m_), [TA, TB], [TB])
                if sample:
                    a3 = a_.rearrange("p (b s) -> p b s", b=NSB)
                    u3 = u_.rearrange("p (b s) -> p b s", b=NSB)
                    h3 = h_.rearrange("p (b s) -> p b s", b=NSB)
                    dve(OP("tensor_mul", out=hls[:, c, :], in0=a3[:, :, 0], in1=h0s[:, c, :]), [TA, h0s], [hls], partial=True)
                    dve(OP("tensor_add", out=u3[:, :, 0], in0=u3[:, :, 0], in1=hls[:, c, :]), [TB, hls], [TB])
                    dve(OP("memset", a3[:, :, 0], 0.0), [], [TA])
                    dve(OP("tensor_tensor_scan", out=h_, data0=a_, data1=u_, initial=0.0, op0=ALU.mult, op1=ALU.add), [TA, TB], [TB])
                    dve(OP("tensor_copy", out=hls[:, c, :], in_=h3[:, :, ST - 1]), [TB], [hls], partial=True)
                else:
                    dve(OP("tensor_tensor_scan", out=h_, data0=a_, data1=u_, initial=hst[:, l, c:c + 1], op0=ALU.mult, op1=ALU.add),
                        [TA, TB, hst], [TB])
                    dve(OP("tensor_copy", out=hst[:, l, c:c + 1], in_=h_[:, T - 1:T]), [TB], [hst])
                dve(OP("tensor_mul", out=yT[:, c, :], in0=yT[:, c, :], in1=h_), [TB, Gc], [Gc], partial=True)
                yield 12.0
            if sample:
                dve(OP("tensor_copy", out=cvs[:], in_=xe4[:, :, :, ST:ST + 3]), [Gb], [cvs])
                store(cvs, dout["scs"][l], cvs[:])
                store(hls, dout["shs"][l], hls[:])
            else:
                dve(OP("tensor_copy", out=ctail[:, l, :, :], in_=xe[:, :, T:T + 3]), [Gb], [ctail])
                if g == 16 // NT - 1:
                    store(ctail, dout["scp"][l], ctail[:, l, :, :])
                    store(hst, dout["shp"][l], hst[:, l, :])
            yield 2.0
        add(None, s_rglru, 98.0)

        def s_ret(_):
            for t in range(nt):
                qkr = Tr[0][:].rearrange("p (h d) -> p h d", h=16)
                qwT = Tr[1][0:64, :].rearrange("p (h j) -> p h j", h=NH)
                kT = Tr[2][0:64, :].rearrange("p (h j) -> p h j", h=NH)
                kwt2 = Tr[3][:].rearrange("p (h d) -> p h d", h=NH)
                kwt = kwt2[:, :, 0:64]
                SmT = Tr[4][:].rearrange("p (h j) -> p h j", h=NH)
                raw = qk_raw[:, t, :].rearrange("p (h d) -> p h d", h=16)
                rc, rs = rot[:, 0, t, :], rot[:, 1, t, :]
                rcr, rsr = rot, rot
                cb_ = rc.unsqueeze(1).to_broadcast([128, 16, 32])
                sb_ = rs.unsqueeze(1).to_broadcast([128, 16, 32])
                x1, x2 = raw[:, :, 0:32], raw[:, :, 32:64]
                o1, o2 = qkr[:, :, 0:32], qkr[:, :, 32:64]
                tmp = Tr[5][:, 0:512].rearrange("p (h d) -> p h d", h=16)
                dve(OP("tensor_tensor", out=o1, in0=x1, in1=cb_, op=ALU.mult), [Gd, rcr], [Tr[0]], partial=True)
                dve(OP("tensor_tensor", out=tmp, in0=x2, in1=sb_, op=ALU.mult), [Gd, rsr], [Tr[5]])
                dve(OP("tensor_tensor", out=o1, in0=o1, in1=tmp, op=ALU.subtract), [Tr[0], Tr[5]], [Tr[0]], partial=True)
                dve(OP("tensor_tensor", out=o2, in0=x1, in1=sb_, op=ALU.mult), [Gd, rsr], [Tr[0]], partial=True)
                dve(OP("tensor_tensor", out=tmp, in0=x2, in1=cb_, op=ALU.mult), [Gd, rcr], [Tr[5]])
                dve(OP("tensor_tensor", out=o2, in0=o2, in1=tmp, op=ALU.add), [Tr[0], Tr[5]], [Tr[0]], partial=True)
                if dbg is not None and dbg.get('sub', 99) < 1:
                    continue
                ps = big()
                for h in range(NH):
                    pe(OP("transpose", out=ps[0:64, h * 128:(h + 1) * 128], in_=qkr[:, h, :], identity=ident[:]), [Tr[0], ident], [ps])
                dve(OP("tensor_tensor", out=qwT, in0=ps[0:64, :].rearrange("p (h j) -> p h j", h=NH), in1=qw[:], op=ALU.mult),
                    [ps, qw], [Tr[1]])
                if dbg is not None and dbg.get('sub', 99) < 2:
                    continue
                ps = big()
                for h in range(NH):
                    pe(OP("transpose", out=ps[0:64, h * 128:(h + 1) * 128], in_=qkr[:, 8 + h, :], identity=ident[:]), [Tr[0], ident], [ps])
                act(OP("activation", out=Tr[2][0:64, :], in_=ps[0:64, :], func=AF.Copy), [ps], [Tr[2]])
                dve(OP("tensor_tensor", out=kwt, in0=qkr[:, 8:16, :], in1=kw[:].unsqueeze(2).to_broadcast([128, NH, 64]), op=ALU.mult),
                    [Tr[0], kw], [Tr[3]])
                if dbg is not None and dbg.get('sub', 99) < 3:
                    continue
                ps = big()
                for h in range(NH):
                    pe(OP("matmul", ps[:, h * 128:(h + 1) * 128], lhsT=kT[:, h, :], rhs=qwT[:, h, :], start=True, stop=True),
                       [Tr[1], Tr[2]], [ps])
                dve(OP("tensor_tensor", out=SmT, in0=ps[:].rearrange("p (h j) -> p h j", h=NH), in1=mask[:], op=ALU.mult),
                    [ps, mask], [Tr[4]])
                if dbg is not None and dbg.get('sub', 99) < 4:
                    continue
                yield 25.0
                pso = big()
                for h in range(NH):
                    pe(OP("matmul", pso[:, h * 128:(h + 1) * 128], lhsT=v_tok[:, t, h * 128:(h + 1) * 128], rhs=SmT[:, h, :],
                                                    start=True, stop=sample), [Ge, Tr[4]], [pso])
                    if not sample:
                        pe(OP("matmul", pso[:, h * 128:(h + 1) * 128], lhsT=Rst[:, l, h, :], rhs=qwT[:, h, :], start=False, stop=True),
                           [Rst, Tr[1]], [pso])
                osb, osq, tm7 = Tr[5], Tr[6], Tr[7]
                act(OP("activation", out=osb[:], in_=pso[:], func=AF.Copy), [pso], [osb])
                if dbg is not None and dbg.get('sub', 99) < 5:
                    continue
                if not sample:
                    psk = big()
                    for h in range(NH):
                        pe(OP("matmul", psk[0:64, h * 128:(h + 1) * 128], lhsT=kwt[:, h, :], rhs=v_tok[:, t, h * 128:(h + 1) * 128],
                                                                 start=True, stop=True), [Tr[3], Ge], [psk])
                    Rl = Rst[:, l, :, :]
                    if dbg is None or dbg.get('sub', 99) >= 7:
                        dve(OP("tensor_tensor", out=Rl, in0=Rl, in1=gc_p[:].unsqueeze(2).to_broadcast([64, NH, 128]), op=ALU.mult), [Rst, gc_p], [Rst])
                    if dbg is None or dbg.get('sub', 99) >= 8:
                        dve(OP("tensor_tensor", out=Rl, in0=psk[0:64, :].rearrange("p (h e) -> p h e", h=NH), in1=Rl, op=ALU.add),
                            [Rst, psk], [Rst])
                    if g == 16 // NT - 1 and t == nt - 1:
                        store(Rst, dout["srp"][l], Rst[:, l, :, :])
                else:
                    R0 = Gb[0:64, 0:NSB * 128].rearrange("p (b e) -> p b e", b=NSB)
                    Vb = Gd[:, 0:NSB * 128].rearrange("p (b e) -> p b e", b=NSB)
                    for h in range(NH):
                        load(Gb, R0, din["sret"][l, h])
                        pc = small()
                        for b in range(NSB):
                            pe(OP("matmul", pc[:, b * ST:(b + 1) * ST], lhsT=R0[:, b, :],
                                                                   rhs=qwT[:, h, b * ST:(b + 1) * ST], start=True, stop=True), [Gb, Tr[1]], [pc])
                        dve(OP("tensor_tensor", out=osb[:, h * 128:(h + 1) * 128], in0=pc[:, 0:128],
                                                                  in1=osb[:, h * 128:(h + 1) * 128], op=ALU.add), [osb, pc], [osb])
                        dve(OP("tensor_tensor", out=Vb, in0=v_tok[:, 0, h * 128:(h + 1) * 128].unsqueeze(1).to_broadcast([128, NSB, 128]),
                                                           in1=oh[:].unsqueeze(2).to_broadcast([128, NSB, 128]), op=ALU.mult), [Ge, oh], [Gd])
                        for q in range(4):
                            pqq = small()
                            pe(OP("matmul", pqq[0:64, :], lhsT=kwt[:, h, :], rhs=Vb[:, q * 4:(q + 1) * 4, :], start=True, stop=True), [Tr[3], Gd], [pqq])
                            dve(OP("tensor_scalar", out=R0[:, q * 4:(q + 1) * 4, :], in0=R0[:, q * 4:(q + 1) * 4, :], scalar1=g8[h], scalar2=None,
                                   op0=ALU.mult), [Gb], [Gb])
                            dve(OP("tensor_tensor", out=R0[:, q * 4:(q + 1) * 4, :], in0=pqq[0:64, :].rearrange("p (b e) -> p b e", b=4),
                                   in1=R0[:, q * 4:(q + 1) * 4, :], op=ALU.add), [Gb, pqq], [Gb])
                        store(Gb, dout["srs"][l, h], R0)
                if dbg is not None and dbg.get('sub', 99) < 9:
                    continue
                yield 25.0
                act(OP("activation", out=osq[:], in_=osb[:], func=AF.Square), [osb], [osq])
                pm, p2 = big(), big()
                for hf in range(2):
                    pe(OP("matmul", pm[:, hf * 512:(hf + 1) * 512], lhsT=onesdiv[:], rhs=osb[:, hf * 512:(hf + 1) * 512], start=True, stop=True),
                       [onesdiv, osb], [pm])
                act(OP("activation", out=tm7[:], in_=pm[:], func=AF.Square), [pm], [tm7])
                if dbg is not None and dbg.get('sub', 99) < 10:
                    continue
                if not (dbg is not None and dbg.get('skipA')):
                    dve(OP("tensor_tensor", out=osb[:], in0=pm[:], in1=osb[:], op=ALU.subtract), [osb, pm], [osb])
                for hf in range(2):
                    pe(OP("matmul", p2[:, hf * 512:(hf + 1) * 512], lhsT=onesdiv[:], rhs=osq[:, hf * 512:(hf + 1) * 512], start=True, stop=True),
                       [onesdiv, osq], [p2])
                if dbg is not None and dbg.get('sub', 99) < 11:
                    continue
                dve(OP("tensor_tensor", out=tm7[:], in0=p2[:], in1=tm7[:], op=ALU.subtract), [p2, tm7], [tm7])
                dve(OP("tensor_scalar_max", out=tm7[:], in0=tm7[:], scalar1=0.0), [tm7], [tm7])
                if dbg is not None and dbg.get('sub', 99) < 12:
                    continue
                act(OP("activation", out=tm7[:], in_=tm7[:], func=AF.Sqrt, bias=EPS), [tm7], [tm7])
                dve(OP("reciprocal", out=tm7[:], in_=tm7[:]), [tm7], [tm7])
                if dbg is not None and dbg.get('sub', 99) < 13:
                    continue
                dve(OP("scalar_tensor_tensor", out=osb[:], in0=osb[:], scalar=-1.0, in1=tm7[:], op0=ALU.mult, op1=ALU.mult), [osb, tm7], [osb])
                if dbg is not None and dbg.get('sub', 99) < 14:
                    continue
                o3 = osb[:].rearrange("p (h j) -> p h j", h=NH)
                dve(OP("tensor_tensor", out=o3, in0=o3, in1=vecs[:, l, 64:72].unsqueeze(2).to_broadcast([128, NH, 128]), op=ALU.mult), [osb, vecs], [osb])
                dve(OP("tensor_tensor", out=zT[:, :, t * 128:(t + 1) * 128], in0=zT[:, :, t * 128:(t + 1) * 128], in1=o3, op=ALU.mult),
                    [Gf, osb], [Gf], partial=True)
                yield 30.0
        add(None, s_ret, 80.0 * nt)

        for bi in range(NBLK):
            def s_ro(w, bi=bi):
                for mi in range(NSUB):
                    ps = small()
                    for kc in range(8):
                        pe(OP("matmul", ps[:, 0:T], lhsT=w[:, kc, mi * 128:(mi + 1) * 128], rhs=zT[:, kc, :],
                                                                  start=(kc == 0), stop=(kc == 7)), [w, Gf], [ps])
                    dst = gaT[:, bi * NSUB + mi, :]
                    dve(OP("tensor_tensor", out=dst, in0=ps[:, 0:T], in1=dst, op=ALU.mult), [ps, Gg], [Gg], partial=True)
            add(din["w_ret"][l][:, bi * BW:(bi + 1) * BW], s_ro)
        for bi in range(NBLK):
            def s_rn(w, bi=bi):
                for mi in range(NSUB):
                    ps = small()
                    for kc in range(8):
                        pe(OP("matmul", ps[:, 0:T], lhsT=w[:, kc, mi * 128:(mi + 1) * 128], rhs=yT[:, kc, :],
                                                                  start=(kc == 0), stop=(kc == 7)), [w, Gc], [ps])
                    dst = gbT[:, bi * NSUB + mi, :]
                    dst2 = gaT[:, bi * NSUB + mi, :]
                    dve(OP("tensor_tensor", out=dst, in0=ps[:, 0:T], in1=dst, op=ALU.mult), [ps, Gh], [Gh], partial=True)
                    dve(OP("tensor_tensor", out=dst2, in0=dst2, in1=dst, op=ALU.add), [Gh, Gg], [Gg], partial=True)
            add(din["w_rnn"][l][:, bi * BW:(bi + 1) * BW], s_rn)
        for bi in range(NBLK):
            def s_wo(w, bi=bi):
                for t in range(nt):
                    ps = small()
                    for kc in range(8):
                        pe(OP("matmul", ps[:, 0:BW], lhsT=gaT[:, kc, t * 128:(t + 1) * 128], rhs=w[:, kc, :],
                                                                start=(kc == 0), stop=(kc == 7)), [w, Gg], [ps])
                    dst = Gx[:, t, bi * BW:(bi + 1) * BW]
                    dve(OP("tensor_tensor", out=dst, in0=ps[:, 0:BW], in1=dst, op=ALU.add), [ps, Gx], [Gx], partial=True)
            add(din["w_o"][l][:, bi * BW:(bi + 1) * BW], s_wo)

        xn2T = Ga[:, 0:8 * T].rearrange("p (c t) -> p c t", c=8)
        xn2 = Ge[:, 0:nt * D].rearrange("p (t d) -> p t d", t=nt)
        qpT = [Gb[:, 0:8 * T].rearrange("p (c t) -> p c t", c=8), Gc[:, 0:8 * T].rearrange("p (c t) -> p c t", c=8)]
        qpR = [Gb, Gc]
        keysT = Gf[:, 0:16 * 128].rearrange("p (a k) -> p a k", a=16)
        sc = Gd[:, 0:16 * 128].rearrange("p (a k) -> p a k", a=16)

        def s_norm2(_):
            for t in range(nt):
                rmsnorm_stats(Gx[:, t, :], Gx, Tr[6])
                dve(OP("scalar_tensor_tensor", out=xn2[:, t, :], in0=Gx[:, t, :], scalar=ssn[:, 2:3], in1=n2g[:, l, :],
                                                         op0=ALU.mult, op1=ALU.mult), [Gx, ssn, n2g], [Ge], partial=True)
                ps = big()
                for c in range(8):
                    pe(OP("transpose", out=ps[:, c * 128:(c + 1) * 128], in_=xn2[:, t, c * 128:(c + 1) * 128], identity=ident[:]),
                       [Ge, ident], [ps])
                act(OP("activation", out=xn2T[:, :, t * 128:(t + 1) * 128], in_=ps[:].rearrange("p (c t) -> p c t", c=8), func=AF.Copy),
                    [ps], [Ga], partial=True)
            load(Gf, keysT, din["keysT"][l])
        add(None, s_norm2, 20.0)
        for bi in range(2048 // BW):
            add(din["wq"][l][:, bi * BW:(bi + 1) * BW],
                lambda w, bi=bi: fm_block(w, T, xn2T, Ga, lambda mi: qpT[(bi * NSUB + mi) // 8][:, (bi * NSUB + mi) % 8, :],
                                          qpR[(bi * NSUB) // 8], AF.Copy))

        def s_peer(_):
            for t in range(nt):
                for half in range(2):
                    ps = big()
                    for a in range(8):
                        pe(OP("matmul", ps[:, a * 128:(a + 1) * 128], lhsT=qpT[half][:, a, t * 128:(t + 1) * 128],
                                                                         rhs=keysT[:, half * 8 + a, :], start=True, stop=True), [qpR[half], Gf], [ps])
                    act(OP("activation", out=Gd[:, half * 1024:(half + 1) * 1024], in_=ps[:], func=AF.Copy), [ps], [Gd], partial=True)
                for a in range(16):
                    for rnd in range(2):
                        sl = slice(rnd * 8, rnd * 8 + 8)
                        dve(OP("max", out=topv[:, a, sl], in_=sc[:, a, :]), [Gd], [topv], partial=True)
                        dve(OP("max_index", out=topi[:, a, sl], in_max=topv[:, a, sl], in_values=sc[:, a, :]), [Gd, topv], [topi], partial=True)
                        if rnd == 0:
                            dve(OP("match_replace", out=sc[:, a, :], in_to_replace=topv[:, a, sl], in_values=sc[:, a, :], imm_value=NEG),
                                [Gd, topv], [Gd])
                dve(OP("tensor_copy", out=topf[:], in_=topi[:]), [topi], [topf])
                yield 40.0
                tv4 = topv[:].rearrange("p (h q) k -> p h q k", q=2)
                tf4 = topf[:].rearrange("p (h q) k -> p h q k", q=2)
                cand = Gg[:, 0:NH * 256].rearrange("p (h c) -> p h c", h=NH)
                cand4 = Gg[:, 0:NH * 256].rearrange("p (h a b) -> p h a b", h=NH, a=16)
                dve(OP("tensor_tensor", out=cand4, in0=tv4[:, :, 0, :].unsqueeze(3).to_broadcast([128, NH, 16, 16]),
                                              in1=tv4[:, :, 1, :].unsqueeze(2).to_broadcast([128, NH, 16, 16]), op=ALU.add), [topv], [Gg])
                for h in range(NH):
                    for rnd in range(2):
                        sl = slice(rnd * 8, rnd * 8 + 8)
                        dve(OP("max", out=b8[:, h, sl], in_=cand[:, h, :]), [Gg], [b8], partial=True)
                        dve(OP("max_index", out=c8[:, h, sl], in_max=b8[:, h, sl], in_values=cand[:, h, :]), [Gg, b8], [c8], partial=True)
                        if rnd == 0:
                            dve(OP("match_replace", out=cand[:, h, :], in_to_replace=b8[:, h, sl], in_values=cand[:, h, :], imm_value=NEG),
                                [Gg, b8], [Gg])
                yield 20.0
                dve(OP("tensor_single_scalar", out=cab[:, 0, :, :], in_=c8[:], scalar=4, op=ALU.logical_shift_right), [c8], [cab], partial=True)
                dve(OP("tensor_single_scalar", out=cab[:, 1, :, :], in_=c8[:], scalar=15, op=ALU.bitwise_and), [c8], [cab], partial=True)
                dve(OP("tensor_copy", out=cabf[:], in_=cab[:]), [cab], [cabf])
                ohA = Gh[:, 0:NH * 256].rearrange("p (h k a) -> p h k a", h=NH, k=16)
                io4 = iota16[:].unsqueeze(1).unsqueeze(1).to_broadcast([128, NH, 16, 16])
                for q in range(2):
                    dve(OP("tensor_tensor", out=ohA, in0=cabf[:, q, :, :].unsqueeze(3).to_broadcast([128, NH, 16, 16]), in1=io4, op=ALU.is_equal),
                        [cabf, iota16], [Gh])
                    dve(OP("tensor_tensor", out=ohA, in0=ohA, in1=tf4[:, :, q, :].unsqueeze(2).to_broadcast([128, NH, 16, 16]), op=ALU.mult),
                        [Gh, topf], [Gh])
                    dve(OP("tensor_reduce", out=isel[:, q, :], in_=Gh[:, 0:NH * 256].rearrange("p (m a) -> p m a", a=16), axis=AX.X, op=ALU.add),
                        [Gh], [isel], partial=True)
                dve(OP("scalar_tensor_tensor", out=isel[:, 2, :], in0=isel[:, 0, :], scalar=128.0, in1=isel[:, 1, :], op0=ALU.mult, op1=ALU.add),
                    [isel], [isel])
                dve(OP("tensor_copy", out=eidxP[par][t][:], in_=isel[:, 2, :]), [isel], [eidxP[par][t]])
                dve(OP("tensor_tensor", out=gsm[:, 0, :, :], in0=b8[:], in1=b8[:, :, 0:1].to_broadcast([128, NH, 16]), op=ALU.subtract), [b8], [gsm], partial=True)
                act(OP("activation", out=gsm[:, 1, :, :], in_=gsm[:, 0, :, :], func=AF.Exp), [gsm], [gsm])
                dve(OP("tensor_reduce", out=gsm[:, 2, :, 0], in_=gsm[:, 1, :, :], axis=AX.X, op=ALU.add), [gsm], [gsm])
                dve(OP("reciprocal", out=gsm[:, 2, :, 1], in_=gsm[:, 2, :, 0]), [gsm], [gsm])
                dve(OP("tensor_tensor", out=gsm[:, 3, :, :], in0=gsm[:, 1, :, :], in1=gsm[:, 2, :, 1:2].to_broadcast([128, NH, 16]), op=ALU.mult), [gsm], [gsm])
                dve(OP("tensor_copy", out=gateP[par][t][:], in_=gsm[:, 3, :, :].rearrange("p h k -> p (h k)")), [gsm], [gateP[par][t]])
                yield 40.0
        add(None, s_peer, 100.0 * nt)

        ms = []

        def gather(tab, t, j):
            gbuf = Gt[j % NG]
            P.op("pool", OP("indirect_dma_start", out=gbuf[:], out_offset=None, in_=tab,
                            in_offset=bass.IndirectOffsetOnAxis(ap=eidxP[par][t][:, j:j + 1], axis=0)),
                 reads=[eidxP[par][t]], writes=[gbuf], dma=gbuf)

        seq = []
        for t in range(nt):
            seq += [("u", t, j) for j in range(128)]
            seq += [("v", t, j) for j in range(128)]

        def issue(si):
            if si < len(seq):
                kind, t, j = seq[si]
                gbuf = Gt[si % NG]
                tab = pu_d[l] if kind == "u" else pv_d[l]
                P.op("pool", OP("indirect_dma_start", out=gbuf[:], out_offset=None, in_=tab,
                                in_offset=bass.IndirectOffsetOnAxis(ap=eidxP[par][t][:, j:j + 1], axis=0)),
                     reads=[eidxP[par][t]], writes=[gbuf], dma=gbuf)

        def m_start():
            for si in range(NG):
                issue(si)
        ms.append(m_start)
        for si, (kind, t, j) in enumerate(seq):
            if kind == "u" and j == 0:
                def m_prep(t=t):
                    rmsnorm_stats(Gx[:, t, :], Gx, Gn, ssn=ssn2)
                    dve(OP("scalar_tensor_tensor", out=Gn[:], in0=Gx[:, t, :], scalar=ssn2[:, 2:3], in1=n2g[:, l, :],
                           op0=ALU.mult, op1=ALU.mult), [Gx, ssn2, n2g], [Gn])
                ms.append(m_prep)
            if kind == "v" and j == 0:
                def m_hid(t=t):
                    act(OP("activation", out=hid[:], in_=hpre[:], func=AF.Gelu_apprx_tanh), [hpre], [hid])
                    dve(OP("tensor_tensor", out=hid[:], in0=hid[:], in1=gateP[par][t][:], op=ALU.mult), [hid, gateP[par][t]], [hid])
                ms.append(m_hid)

            def m_slot(si=si, kind=kind, t=t, j=j):
                gbuf = Gt[si % NG]
                if kind == "u":
                    dve(OP("scalar_tensor_tensor", out=gbuf[:], in0=gbuf[:], scalar=1.0, in1=Gn[:], op0=ALU.mult, op1=ALU.mult,
                           accum_out=hpre[:, j:j + 1]), [gbuf, Gn], [gbuf, hpre], partial=True)
                else:
                    dve(OP("scalar_tensor_tensor", out=Gx[:, t, :], in0=gbuf[:], scalar=hid[:, j:j + 1], in1=Gx[:, t, :], op0=ALU.mult, op1=ALU.add),
                        [gbuf, hid, Gx], [Gx], partial=True)
                issue(si + NG)
            ms.append(m_slot)
        if l == DEPTH - 1:
            def m_out():
                nf = Gt[NG - 1]
                load(nf, nf[:], din["nfg"].partition_broadcast(128))
                for t in range(nt):
                    yb = Gt[t % 2]
                    rmsnorm_stats(Gx[:, t, :], Gx, Gn, ssn=ssn2)
                    dve(OP("scalar_tensor_tensor", out=yb[:], in0=Gx[:, t, :], scalar=ssn2[:, 2:3], in1=nf[:], op0=ALU.mult, op1=ALU.mult),
                        [Gx, ssn2, nf], [yb])
                    if sample:
                        store(yb, dout["ys"], yb[:])
                    else:
                        r0 = g * TG + t * 128
                        store(yb, dout["yp"][r0:r0 + 128, :], yb[:])
            ms.append(m_out)
        return steps, ms

    ngroups = 16 // NT
    items = []
    for a in range(0, ngroups, 2):
        for l in range(DEPTH):
            items.append((a, l, False))
            items.append((a + 1, l, False))
    for l in range(DEPTH):
        items.append((ngroups, l, True))

    import types
    wcount = [0]
    pending = []

    def run_s2(n):
        while n > 0 and pending:
            pending.pop(0)()
            n -= 1

    nstep_total = 0
    for k, (g, l, sample) in enumerate(items):
        par = k % 2
        Gx = GxP[g % 2]
        if l == 0:
            if g == 0:
                load(mask, mask[:], din["mask_p"])
                load(qw, qw[:], din["qw_p"])
                load(kw, kw[:], din["kw_p"])
            if sample:
                load(mask, mask[:], din["mask_s"])
                load(qw, qw[:], din["qw_s"])
                load(kw, kw[:], din["kw_s"])
                load(Gx, Gx[:, 0, :], din["xs"])
            else:
                load(Gx, Gx[:], din["xp"][g * TG:(g + 1) * TG, :].rearrange("(t p) d -> p t d", p=128))
        if k > 0 and items[k - 1][0] == g:
            run_s2(len(pending))
        s1, s2 = group_layer(g, l, sample, par)
        tot = sum(e for _, _, e in s1)
        rate = 1.0 / 1.5
        debt = 0.0
        blks = [i for i, (b_, _, _) in enumerate(s1) if b_ is not None]
        loaded = {}
        nxt = 0
        bpos = 0
        stop_all = False
        for i, (blk, fn, est) in enumerate(s1):
            if dbg is not None and nstep_total >= dbg["stop"]:
                stop_all = True
                break
            nstep_total += 1
            want = bpos + (NWB - 1 if blk is not None else NWB - 2)
            while nxt < len(blks) and nxt <= want:
                wreg = wbuf[wcount[0] % NWB]
                wcount[0] += 1
                load(wreg, wreg[:], wview(s1[blks[nxt]][0]))
                loaded[blks[nxt]] = wreg
                nxt += 1
            if blk is not None:
                debt += est * rate
                n = int(debt)
                debt -= n
                run_s2(n)
            hook["on"] = blk is None
            hook["fn"] = run_s2
            r = fn(loaded.get(i))
            if blk is not None:
                bpos += 1
            if isinstance(r, types.GeneratorType):
                for e in r:
                    pass
            hook["on"] = False
        if stop_all:
            break
        run_s2(len(pending))
        pending = list(s2)
    if dbg is None or not stop_all:
        run_s2(len(pending))

    if dbg is not None:
        allregs = {"Gx": GxP[0], "GxB": GxP[1], "Ga": Ga, "Gb": Gb, "Gc": Gc, "Gd": Gd, "Ge": Ge, "Gf": Gf, "Gg": Gg, "Gh": Gh,
                   "Rst": Rst, "hst": hst, "ctail": ctail, "ccol": ccol, "hpre": hpre, "hid": hid, "gsm": gsm,
                   "isel": isel, "topv": topv, "topf": topf, "b8": b8, "cabf": cabf}
        for i in range(8):
            allregs[f"T{i}"] = Tr[i]
        for nm in dbg.get("dump", []):
            reg = allregs[nm]
            shp = list(reg.shape)
            d = nc.dram_tensor("dbg_" + nm, shp, F32, kind="ExternalOutput").ap()
            store(reg, d, reg[:])
    P.finish()
    P.emit()
    P.stack.close()
    return nc


_CACHE = {}


def _f32(a):
    return np.ascontiguousarray(np.asarray(a), dtype=np.float32)


def kernel(x_prompt, x_sample, state_ret, state_rnn, state_conv, norm1_g, norm2_g, normf_g, w_in,
           ret_gn_g, w_ret_out, conv_w, conv_b, rg_wa, rg_ba, rg_wx, rg_bx, rg_lambda, w_rnn_out, w_o,
           peer_wq, peer_keys, peer_u, peer_v):
    consts, g8 = host_consts()
    if "nc" not in _CACHE:
        _CACHE["nc"] = build_program(g8)
    nc = _CACHE["nc"]

    x_prompt = _f32(x_prompt); x_sample = _f32(x_sample)
    state_ret = _f32(state_ret); state_rnn = _f32(state_rnn); state_conv = _f32(state_conv)
    peer_u = _f32(peer_u); peer_v = _f32(peer_v)

    def fm(v):
        return _f32(v).reshape(DEPTH, 8, 128).transpose(0, 2, 1)

    vecs = np.zeros((DEPTH, 128, NV), np.float32)
    vecs[:, :, 0:32] = _f32(conv_w).reshape(DEPTH, 4, 8, 128).transpose(0, 3, 2, 1).reshape(DEPTH, 128, 32)
    vecs[:, :, 32:40] = fm(conv_b)
    vecs[:, :, 40:48] = fm(rg_ba)
    vecs[:, :, 48:56] = fm(rg_bx)
    vecs[:, :, 56:64] = fm(rg_lambda)
    vecs[:, :, 64:72] = fm(ret_gn_g)
    vecs[:, :, 72:80] = fm(norm1_g)
    wab = np.zeros((DEPTH, 128, 2, 8, 128), np.float32)
    for qi, wsrc in enumerate((_f32(rg_wa), _f32(rg_wx))):
        for nl in range(2):
            blk = wsrc[:, nl::2]
            wab[:, nl * 64:(nl + 1) * 64, qi, :, nl * 64:(nl + 1) * 64] = blk.transpose(0, 2, 1, 3)
    keysT = np.ascontiguousarray(_f32(peer_keys).reshape(DEPTH, 16, 128, 128).transpose(0, 3, 1, 2))
    shared = {
        "w_in": _f32(w_in), "w_ret": _f32(w_ret_out), "w_rnn": _f32(w_rnn_out), "w_o": _f32(w_o),
        "wq": _f32(peer_wq), "keysT": keysT,
        "pu0": peer_u[0], "pu1": peer_u[1], "pv0": peer_v[0], "pv1": peer_v[1],
        "vecs": vecs, "n2g": _f32(norm2_g), "nfg": _f32(normf_g), "wab": wab,
    }
    shared.update(consts)
    in_maps = []
    for c in range(NCORES):
        sl = slice(c * NSB, (c + 1) * NSB)
        m = dict(shared)
        m["xp"] = x_prompt[c]
        m["xs"] = np.ascontiguousarray(x_sample[sl].reshape(128, D))
        m["sret"] = np.ascontiguousarray(state_ret[:, sl].transpose(0, 2, 3, 1, 4))
        m["sconvT"] = np.ascontiguousarray(state_conv[:, sl].reshape(DEPTH, NSB, 3, 8, 128).transpose(0, 4, 3, 1, 2))
        m["srnnT"] = np.ascontiguousarray(state_rnn[:, sl].reshape(DEPTH, NSB, 8, 128).transpose(0, 3, 2, 1))
        in_maps.append(m)
    res = run_bass_kernel_spmd(nc, in_maps, core_ids=list(range(NCORES)))
    outs = res.results

    y_p = np.stack([outs[c]["yp"] for c in range(NCORES)], 0)
    y_s = np.concatenate([outs[c]["ys"].reshape(NSB, ST, D) for c in range(NCORES)], 0)
    sr_p = np.stack([outs[c]["srp"].transpose(0, 2, 1, 3) for c in range(NCORES)], 1)
    sh_p = np.stack([outs[c]["shp"].transpose(0, 2, 1).reshape(DEPTH, D) for c in range(NCORES)], 1)
    sc_p = np.stack([outs[c]["scp"].transpose(0, 3, 2, 1).reshape(DEPTH, 3, D) for c in range(NCORES)], 1)
    sr_s = np.concatenate([outs[c]["srs"].transpose(0, 3, 1, 2, 4) for c in range(NCORES)], 1)
    sh_s = np.concatenate([outs[c]["shs"].transpose(0, 3, 2, 1).reshape(DEPTH, NSB, D) for c in range(NCORES)], 1)
    sc_s = np.concatenate([outs[c]["scs"].transpose(0, 3, 4, 2, 1).reshape(DEPTH, NSB, 3, D) for c in range(NCORES)], 1)
    f = lambda a: np.ascontiguousarray(a, dtype=np.float32)
    return (f(y_p), f(y_s), f(sr_p), f(sh_p), f(sc_p), f(sr_s), f(sh_s), f(sc_s))
```
